# Optimizing a Trainium2 kernel written in Bass

```python
import math
import numpy as np
import jax
import jax.numpy as jnp
from jax import lax

D_MODEL = 1024
BATCH = 16
SEQ = 2048
DEPTH = 4

HEAD_DIM = 64
Q_BLOCK = 128
EPS = 1e-6
NEG = -1e30
FORCE_SCORE = 1e4

SB_HEADS = 8
SB_W = SB_HEADS * HEAD_DIM
NSA_HEADS = 8
NSA_GROUPS = 2
NSA_HPG = NSA_HEADS // NSA_GROUPS
NSA_W = NSA_HEADS * HEAD_DIM
NSA_KV_W = NSA_GROUPS * HEAD_DIM
CMP_LEN = 32
CMP_STRIDE = 16
SEL_BLOCK = 64
SEL_TOPN = 4
WINDOW = 256
HGRN_HEADS = 4
HGRN_DK = 128
HGRN_DV = 128
HGRN_K_W = HGRN_HEADS * HGRN_DK
HGRN_V_W = HGRN_HEADS * HGRN_DV
HGRN_CHUNK = 64
N_BRANCH = 3
IN_SPLITS = (SB_W, SB_W, SB_W, NSA_W, 6 * NSA_KV_W, 3 * NSA_HEADS, HGRN_K_W, HGRN_K_W, HGRN_V_W, HGRN_V_W, N_BRANCH * D_MODEL)
IN_WIDTH = sum(IN_SPLITS)
PEER_HEADS = 8
PEER_NKEYS = 128
PEER_EXPERTS = PEER_NKEYS * PEER_NKEYS
PEER_TOPK = 16
PEER_QDIM = 128
PEER_TOK_CHUNK = 128

kernel_name = "hybrid_sb_nsa_hgrn2_peer"


def rms_norm(x, g):
    xf = x.astype(jnp.float32)
    y = xf * lax.rsqrt(jnp.mean(xf * xf, axis=-1, keepdims=True) + EPS)
    return (y * g.astype(jnp.float32)).astype(x.dtype)


def alibi_slopes(n_heads):
    return jnp.exp2(-8.0 * (jnp.arange(n_heads, dtype=jnp.float32) + 1.0) / n_heads)


def stick_breaking_attention(q, k, v):
    B, H, T, dh = q.shape
    scale = dh ** -0.5
    outs = []
    for i in range(T // Q_BLOCK):
        q0 = i * Q_BLOCK
        kl = q0 + Q_BLOCK
        z = jnp.einsum("bhtd,bhsd->bhts", q[:, :, q0:kl], k[:, :, :kl]).astype(jnp.float32) * scale
        before = jnp.arange(kl)[None, :] < (q0 + jnp.arange(Q_BLOCK))[:, None]
        log_1m = jnp.where(before, jax.nn.log_sigmoid(-z), 0.0)
        after = lax.cumsum(log_1m, axis=3, reverse=True) - log_1m
        a = jnp.where(before, jnp.exp(jax.nn.log_sigmoid(z) + after), 0.0)
        outs.append(jnp.einsum("bhts,bhsd->bhtd", a.astype(v.dtype), v[:, :, :kl]))
    return jnp.concatenate(outs, axis=2)


def nsa_attention(q, k_cmp, v_cmp, k_sel, v_sel, k_win, v_win, gates, w_cmp_k, w_cmp_v, cmp_pe):
    B, G, P, T, dh = q.shape
    f32 = jnp.float32
    scale = dh ** -0.5
    slopes = alibi_slopes(G * P).reshape(G, P)
    t_pos = jnp.arange(T)

    n_piece = T // CMP_STRIDE
    per = CMP_LEN // CMP_STRIDE
    n_cmp = n_piece - per + 1

    def compress(x, w):
        pieces = x.reshape(B, G, n_piece, CMP_STRIDE, dh)
        blocks = jnp.concatenate([pieces[:, :, j:j + n_cmp] for j in range(per)], axis=3) + cmp_pe
        return blocks.reshape(B, G, n_cmp, CMP_LEN * dh) @ w

    kc = compress(k_cmp, w_cmp_k)
    vc = compress(v_cmp, w_cmp_v)
    c_start = jnp.arange(n_cmp) * CMP_STRIDE
    dist_c = t_pos[:, None] - (c_start + CMP_LEN - 1)[None, :]
    valid_c = dist_c >= 0
    s_c = jnp.einsum("bgptd,bgcd->bgptc", q, kc).astype(f32) * scale - slopes[:, :, None, None] * dist_c.astype(f32)
    p_c = jnp.where(valid_c, jax.nn.softmax(jnp.where(valid_c, s_c, NEG), axis=-1), 0.0)
    o_c = jnp.einsum("bgptc,bgcd->bgptd", p_c.astype(vc.dtype), vc)

    n_blk = T // SEL_BLOCK
    n_sel = min(SEL_TOPN, n_blk)
    b_start = jnp.arange(n_blk) * SEL_BLOCK
    overlap = ((c_start[:, None] < b_start[None, :] + SEL_BLOCK) & (c_start[:, None] + CMP_LEN > b_start[None, :])).astype(f32)
    imp = jnp.einsum("bgptc,cn->bgtn", p_c, overlap)
    bid = jnp.arange(n_blk)
    forced = (bid[None, :] == (t_pos // SEL_BLOCK)[:, None]) | (bid[None, :] == 0)
    causal_b = b_start[None, :] <= t_pos[:, None]
    imp = jnp.where(forced, FORCE_SCORE, jnp.where(causal_b, imp, NEG))
    _, sel_idx = lax.top_k(imp, n_sel)

    kblk = k_sel.reshape(B, G, n_blk, SEL_BLOCK, dh)
    vblk = v_sel.reshape(B, G, n_blk, SEL_BLOCK, dh)
    nq = T // Q_BLOCK
    q_chunks = jnp.moveaxis(q.reshape(B, G, P, nq, Q_BLOCK, dh), 3, 0)
    idx_chunks = jnp.moveaxis(sel_idx.reshape(B, G, nq, Q_BLOCK, n_sel), 2, 0)
    bi = jnp.arange(B)[:, None, None, None]
    gi = jnp.arange(G)[None, :, None, None]
    n_keys_sel = n_sel * SEL_BLOCK

    def sel_chunk(args):
        qc, ic, c = args
        kg = kblk[bi, gi, ic].reshape(B, G, Q_BLOCK, n_keys_sel, dh)
        vg = vblk[bi, gi, ic].reshape(B, G, Q_BLOCK, n_keys_sel, dh)
        pos = (ic[..., None] * SEL_BLOCK + jnp.arange(SEL_BLOCK)).reshape(B, G, Q_BLOCK, n_keys_sel)
        dist = (c * Q_BLOCK + jnp.arange(Q_BLOCK))[:, None] - pos
        sc = jnp.einsum("bgpqd,bgqsd->bgpqs", qc, kg).astype(f32) * scale - slopes[None, :, :, None, None] * dist[:, :, None].astype(f32)
        pr = jax.nn.softmax(jnp.where(dist[:, :, None] >= 0, sc, NEG), axis=-1)
        return jnp.einsum("bgpqs,bgqsd->bgpqd", pr.astype(vg.dtype), vg)

    o_s = lax.map(sel_chunk, (q_chunks, idx_chunks, jnp.arange(nq)))
    o_s = jnp.moveaxis(o_s, 0, 3).reshape(B, G, P, T, dh)

    nb = T // Q_BLOCK
    nwb = WINDOW // Q_BLOCK
    span = (nwb + 1) * Q_BLOCK

    def band(x):
        xb = jnp.pad(x.reshape(B, G, nb, Q_BLOCK, dh), ((0, 0), (0, 0), (nwb, 0), (0, 0), (0, 0)))
        return jnp.concatenate([xb[:, :, j:j + nb] for j in range(nwb + 1)], axis=3)

    kwb = band(k_win)
    vwb = band(v_win)
    tq = t_pos.reshape(nb, Q_BLOCK)
    sk = jnp.arange(nb)[:, None] * Q_BLOCK - nwb * Q_BLOCK + jnp.arange(span)[None, :]
    dist_w = tq[:, :, None] - sk[:, None, :]
    valid_w = (dist_w >= 0) & (dist_w < WINDOW) & (sk[:, None, :] >= 0)
    s_w = jnp.einsum("bgpnqd,bgnsd->bgpnqs", q.reshape(B, G, P, nb, Q_BLOCK, dh), kwb).astype(f32) * scale - slopes[:, :, None, None, None] * dist_w.astype(f32)
    p_w = jax.nn.softmax(jnp.where(valid_w, s_w, NEG), axis=-1)
    o_w = jnp.einsum("bgpnqs,bgnsd->bgpnqd", p_w.astype(vwb.dtype), vwb).reshape(B, G, P, T, dh)

    g = jax.nn.sigmoid(gates.astype(f32)).reshape(B, T, G, P, 3).transpose(4, 0, 2, 3, 1)
    o = g[0][..., None] * o_c + g[1][..., None] * o_s + g[2][..., None] * o_w
    return o.astype(q.dtype)


def hgrn2_recurrence(q, log_f, k, i):
    B, H, T, dk = q.shape
    dv = i.shape[-1]
    C = HGRN_CHUNK
    nc = T // C

    def to_chunks(x):
        return jnp.moveaxis(x.reshape(B, H, nc, C, x.shape[-1]), 2, 0)

    causal = jnp.tril(jnp.ones((C, C), dtype=bool))

    def step(S, inp):
        qc, gc, kc, ic = inp
        b = jnp.cumsum(gc, axis=2)
        o_inter = jnp.einsum("bhtk,bhkv->bhtv", qc * jnp.exp(b), S)
        rel = b[:, :, :, None, :] - b[:, :, None, :, :]
        decay = jnp.exp(jnp.where(causal[:, :, None], rel, -jnp.inf))
        att = jnp.einsum("bhtk,bhsk,bhtsk->bhts", qc, kc, decay)
        o = o_inter + jnp.einsum("bhts,bhsv->bhtv", att, ic)
        b_last = b[:, :, -1:, :]
        S_new = jnp.exp(b_last[:, :, 0, :, None]) * S + jnp.einsum("bhsk,bhsv->bhkv", kc * jnp.exp(b_last - b), ic)
        return S_new, o

    S0 = jnp.zeros((B, H, dk, dv), dtype=jnp.float32)
    _, o = lax.scan(step, S0, (to_chunks(q), to_chunks(log_f), to_chunks(k), to_chunks(i)))
    return jnp.moveaxis(o, 0, 2).reshape(B, H, T, dv)


def hybrid_mixer(h, w_in, w_cmp_k, w_cmp_v, cmp_pe, hgrn_norm, lower, w_sb, w_nsa, w_hg, w_out):
    B, T, _ = h.shape
    f32 = jnp.float32
    proj = h @ w_in
    offsets = np.cumsum(IN_SPLITS)[:-1].tolist()
    sb_q, sb_k, sb_v, nsa_q, nsa_kv, nsa_g, hg_q, hg_f, hg_i, hg_g, merge_g = jnp.split(proj, offsets, axis=-1)

    def heads(t, n):
        return t.reshape(B, T, n, -1).transpose(0, 2, 1, 3)

    o_sb = stick_breaking_attention(heads(sb_q, SB_HEADS), heads(sb_k, SB_HEADS), heads(sb_v, SB_HEADS))
    o_sb = o_sb.transpose(0, 2, 1, 3).reshape(B, T, SB_W)

    qn = heads(nsa_q, NSA_HEADS).reshape(B, NSA_GROUPS, NSA_HPG, T, HEAD_DIM)
    kv = nsa_kv.reshape(B, T, 6, NSA_GROUPS, HEAD_DIM).transpose(2, 0, 3, 1, 4)
    o_nsa = nsa_attention(qn, kv[0], kv[1], kv[2], kv[3], kv[4], kv[5], nsa_g.reshape(B, T, NSA_HEADS, 3), w_cmp_k, w_cmp_v, cmp_pe)
    o_nsa = o_nsa.reshape(B, NSA_HEADS, T, HEAD_DIM).transpose(0, 2, 1, 3).reshape(B, T, NSA_W)

    lb = lower.reshape(HGRN_HEADS, 1, HGRN_DK)
    f = lb + (1.0 - lb) * jax.nn.sigmoid(heads(hg_f, HGRN_HEADS).astype(f32))
    qh = jax.nn.silu(heads(hg_q, HGRN_HEADS).astype(f32))
    o_hg = hgrn2_recurrence(qh, jnp.log(f), 1.0 - f, heads(hg_i, HGRN_HEADS).astype(f32))
    o_hg = rms_norm(o_hg.transpose(0, 2, 1, 3), hgrn_norm.reshape(HGRN_HEADS, HGRN_DV))
    o_hg = (o_hg * jax.nn.silu(hg_g.astype(f32).reshape(B, T, HGRN_HEADS, HGRN_DV))).reshape(B, T, HGRN_V_W).astype(h.dtype)

    gm = jax.nn.sigmoid(merge_g.astype(f32)).reshape(B, T, N_BRANCH, D_MODEL)
    m = gm[:, :, 0] * (o_sb @ w_sb) + gm[:, :, 1] * (o_nsa @ w_nsa) + gm[:, :, 2] * (o_hg @ w_hg)
    return m.astype(h.dtype) @ w_out


def peer_ffn(x, w_q, sub_keys, u_tab, v_tab):
    B, T, D = x.shape
    f32 = jnp.float32
    Ct = PEER_TOK_CHUNK
    H, K = PEER_HEADS, PEER_TOPK
    xt = x.reshape((B * T) // Ct, Ct, D)

    def chunk(xc):
        qh = (xc @ w_q).reshape(Ct, H, 2, PEER_QDIM // 2)
        s = jnp.einsum("nhad,akd->nhak", qh, sub_keys).astype(f32)
        top_s, top_i = lax.top_k(s, K)
        cand = (top_s[:, :, 0, :, None] + top_s[:, :, 1, None, :]).reshape(Ct, H, K * K)
        cand_idx = (top_i[:, :, 0, :, None] * PEER_NKEYS + top_i[:, :, 1, None, :]).reshape(Ct, H, K * K)
        best_s, pos = lax.top_k(cand, K)
        eidx = jnp.take_along_axis(cand_idx, pos, axis=-1)
        gate = jax.nn.softmax(best_s, axis=-1)
        ug = u_tab[eidx]
        vg = v_tab[eidx]
        hpre = jnp.einsum("nd,nhkd->nhk", xc, ug).astype(f32)
        act = (gate * jax.nn.gelu(hpre, approximate=False)).astype(x.dtype)
        return jnp.einsum("nhk,nhkd->nd", act, vg)

    return lax.map(chunk, xt).reshape(B, T, D)


def setup_inputs(seed: int = 0) -> dict:
    key = jax.random.key(seed)
    ks = jax.random.split(key, 20)
    f32 = jnp.float32
    nrm = lambda k, shape, s: jax.random.normal(k, shape, dtype=f32) * s
    L, D = DEPTH, D_MODEL
    return {
        "x": nrm(ks[0], (BATCH, SEQ, D), 1.0),
        "norm_mix": 1.0 + nrm(ks[1], (L, D), 0.02),
        "norm_ffn": 1.0 + nrm(ks[2], (L, D), 0.02),
        "w_in": nrm(ks[3], (L, D, IN_WIDTH), D ** -0.5),
        "nsa_w_cmp_k": nrm(ks[4], (L, CMP_LEN * HEAD_DIM, HEAD_DIM), (CMP_LEN * HEAD_DIM) ** -0.5),
        "nsa_w_cmp_v": nrm(ks[5], (L, CMP_LEN * HEAD_DIM, HEAD_DIM), (CMP_LEN * HEAD_DIM) ** -0.5),
        "nsa_cmp_pe": nrm(ks[6], (L, CMP_LEN, HEAD_DIM), 0.1),
        "hgrn_norm": 1.0 + nrm(ks[7], (L, HGRN_V_W), 0.02),
        "hgrn_lower_bounds": nrm(ks[8], (L, HGRN_K_W), 0.1),
        "w_branch_sb": nrm(ks[9], (L, SB_W, D), SB_W ** -0.5),
        "w_branch_nsa": nrm(ks[10], (L, NSA_W, D), NSA_W ** -0.5),
        "w_branch_hgrn": nrm(ks[11], (L, HGRN_V_W, D), HGRN_V_W ** -0.5),
        "w_out": nrm(ks[12], (L, D, D), 0.5 * D ** -0.5),
        "peer_w_q": nrm(ks[13], (L, D, PEER_HEADS * PEER_QDIM), D ** -0.5),
        "peer_sub_keys": nrm(ks[14], (L, 2, PEER_NKEYS, PEER_QDIM // 2), (PEER_QDIM // 2) ** -0.5),
        "peer_u": nrm(ks[15], (L, PEER_EXPERTS, D), D ** -0.5),
        "peer_v": nrm(ks[16], (L, PEER_EXPERTS, D), (PEER_HEADS * PEER_TOPK) ** -0.5),
        "norm_final": 1.0 + nrm(ks[17], (D,), 0.02),
    }


def reference(x, norm_mix, norm_ffn, w_in, nsa_w_cmp_k, nsa_w_cmp_v, nsa_cmp_pe, hgrn_norm, hgrn_lower_bounds, w_branch_sb, w_branch_nsa, w_branch_hgrn, w_out, peer_w_q, peer_sub_keys, peer_u, peer_v, norm_final):
    lb_soft = jax.nn.softmax(hgrn_lower_bounds.astype(jnp.float32), axis=0)
    lower = jnp.cumsum(lb_soft, axis=0) - lb_soft[0]
    for l in range(DEPTH):
        h = rms_norm(x, norm_mix[l])
        x = x + hybrid_mixer(h, w_in[l], nsa_w_cmp_k[l], nsa_w_cmp_v[l], nsa_cmp_pe[l], hgrn_norm[l], lower[l], w_branch_sb[l], w_branch_nsa[l], w_branch_hgrn[l], w_out[l])
        h = rms_norm(x, norm_ffn[l])
        x = x + peer_ffn(h, peer_w_q[l], peer_sub_keys[l], peer_u[l], peer_v[l])
    return rms_norm(x, norm_final)
```

```python
from contextlib import ExitStack
import numpy as np
import concourse.bass as bass
import concourse.mybir as mybir
from concourse.bass import IndirectOffsetOnAxis
from concourse.bass_utils import run_bass_kernel_spmd

F32 = mybir.dt.float32
F32R = mybir.dt.float32r
BF16 = mybir.dt.bfloat16
U32 = mybir.dt.uint32
I32 = mybir.dt.int32
AF = mybir.ActivationFunctionType
ALU = mybir.AluOpType
AX = mybir.AxisListType

D = 1024
EPS = 1e-6
NEG = -1e30
FORCE = 1e4
IN_SPLITS = (512, 512, 512, 512, 768, 24, 512, 512, 512, 512, 3072)
IN_W = sum(IN_SPLITS)
NDS = 12


class Res:
    __slots__ = ("w", "r", "name")

    def __init__(self, name=""):
        self.w = None
        self.r = {}
        self.name = name


class Ctx:
    def __init__(self, nc, stack):
        self.nc = nc
        self.eng = {"pe": nc.tensor, "dve": nc.vector, "act": nc.scalar, "pool": nc.gpsimd, "sp": nc.sync}
        self.sem = {e: stack.enter_context(nc.semaphore("s_" + e)) for e in self.eng}
        self.cnt = {e: 0 for e in self.eng}
        self.seen = {e: {} for e in self.eng}
        self.dq = {q: [stack.enter_context(nc.semaphore("d_%s%d" % (q, i))) for i in range(NDS)] for q in ("sp", "act", "pool")}
        self.dqi = {q: 0 for q in self.dq}
        self.dq_uses = {q: [0] * NDS for q in self.dq}
        self.n_ins = 0
        self.uid = 0

    @staticmethod
    def _add(deps, st):
        key, sem, val = st
        cur = deps.get(key)
        if cur is None or cur[1] < val:
            deps[key] = (sem, val)

    def _deps(self, reads, writes):
        deps = {}
        for r in reads:
            if r.w is not None:
                self._add(deps, r.w)
        for w in writes:
            if w.w is not None:
                self._add(deps, w.w)
            for key, (sem, val) in w.r.items():
                self._add(deps, (key, sem, val))
        return deps

    def _wait(self, e, deps):
        seen = self.seen[e]
        for key, (sem, val) in deps.items():
            if key == "pe" and e == "pe":
                continue
            if seen.get(key, 0) >= val:
                continue
            self.eng[e].wait_ge(sem, val)
            seen[key] = val
            self.n_ins += 1

    def _stamp(self, st, reads, writes):
        key, sem, val = st
        for w in writes:
            w.w = st
            w.r = {}
        for r in reads:
            if r in writes:
                continue
            cur = r.r.get(key)
            if cur is None or cur[1] < val:
                r.r[key] = (sem, val)

    def op(self, e, fn, reads=(), writes=()):
        deps = self._deps(reads, writes)
        self._wait(e, deps)
        ins = fn(self.eng[e])
        self.cnt[e] += 1
        ins.then_inc(self.sem[e], 1)
        self.n_ins += 1
        self._stamp((e, self.sem[e], self.cnt[e]), reads, writes)
        return ins

    def dma(self, q, out, in_, reads=(), writes=(), indirect=None, **kw):
        deps = self._deps(reads, writes)
        i = self.dqi[q] % NDS
        self.dqi[q] += 1
        sem = self.dq[q][i]
        prev = self.dq_uses[q][i]
        key = ("d", q, i)
        if prev > 0:
            self._add(deps, (key, sem, 16 * prev))
        self._wait(q, deps)
        if indirect is not None:
            ins = self.eng[q].indirect_dma_start(out, None, in_, indirect, **kw)
        else:
            ins = self.eng[q].dma_start(out=out, in_=in_, **kw)
        ins.then_inc(sem, 16)
        self.n_ins += 1
        self.dq_uses[q][i] = prev + 1
        self._stamp((key, sem, 16 * (prev + 1)), reads, writes)
        return ins

    def barrier(self):
        deps = {}
        for e in self.eng:
            if self.cnt[e] > 0:
                deps[e] = (self.sem[e], self.cnt[e])
        for q in self.dq:
            for i in range(NDS):
                if self.dq_uses[q][i] > 0:
                    deps[("d", q, i)] = (self.dq[q][i], 16 * self.dq_uses[q][i])
        for e in self.eng:
            seen = self.seen[e]
            for key, (sem, val) in deps.items():
                if key == e or seen.get(key, 0) >= val:
                    continue
                self.eng[e].wait_ge(sem, val)
                seen[key] = val
                self.n_ins += 1

    def sb(self, stack, name, shape, dtype):
        self.uid += 1
        t = stack.enter_context(self.nc.sbuf_tensor("sb%d_%s" % (self.uid, name), list(shape), dtype))
        return t, Res(name)

    def ps(self, stack, name, shape, dtype=F32):
        self.uid += 1
        t = stack.enter_context(self.nc.psum_tensor("ps%d_%s" % (self.uid, name), list(shape), dtype))
        return t, Res(name)


class Cfg:
    def __init__(self, T=2048, NB=2, L=4):
        self.T = T
        self.NB = NB
        self.L = L
        self.N = NB * T
        self.NT = self.N // 128
        self.NCMP = T // 16 - 1
        self.NBLK = T // 64
        self.NCH = T // 64


def make_consts(cfg):
    T = cfg.T
    f = np.float32
    k = {}
    k["c_ident"] = np.eye(128, dtype=f)
    j = np.arange(128)
    k["c_ustr"] = (j[:, None] > j[None, :]).astype(f)
    ncmp, nblk = cfg.NCMP, cfg.NBLK
    cs = np.arange(ncmp) * 16
    bs = np.arange(nblk) * 64
    ov = np.zeros((128, 32), f)
    ov[:ncmp, :nblk] = ((cs[:, None] < bs[None, :] + 64) & (cs[:, None] + 32 > bs[None, :])).astype(f)
    k["c_overlap"] = ov
    s = np.arange(64)
    tm = (s[None, :] >= s[:, None]).astype(f)
    k["c_trimask"] = np.ascontiguousarray(np.broadcast_to(tm[:, None, :], (64, 8, 64))).reshape(64, 512)
    eg = np.zeros((24, 24, 128), f)
    for r in range(24):
        eg[r, r, :] = 1.0
    k["c_egate"] = eg.reshape(24, 24 * 128)
    BIGM = 1024.0
    es = np.zeros((33, 16, 128), f)
    for kb in range(16):
        es[2 * kb, kb, 0:64] = BIGM
        es[2 * kb + 1, kb, 64:128] = BIGM
    es[32, :, :] = -BIGM
    k["c_esel"] = es.reshape(33, 16 * 128)
    t = np.arange(T)
    slopes = 2.0 ** (-(np.arange(8) + 1.0))
    qa = np.zeros((4, 8, T), f)
    for h in range(8):
        sl = slopes[h] * 8.0
        qa[0, h] = 128.0 * sl
        qa[1, h] = sl
        qa[2, h] = -128.0 * sl * (t // 128)
        qa[3, h] = -sl * (t % 128)
    k["c_qaug"] = qa.reshape(4, 8 * T)
    ka = np.zeros((4, T), f)
    ka[0] = t // 128
    ka[1] = t % 128
    ka[2] = 1.0
    ka[3] = 1.0
    k["c_kaug"] = ka
    pc = np.arange(128) * 16 + 31
    kc = np.zeros((4, 128), f)
    kc[0] = pc // 128
    kc[1] = pc % 128
    kc[2] = 1.0
    kc[3] = 1.0
    k["c_kaugc"] = kc
    return k


CONST_SHAPES = lambda cfg: {
    "c_ident": [128, 128], "c_ustr": [128, 128], "c_overlap": [128, 32], "c_trimask": [64, 512],
    "c_egate": [24, 24 * 128], "c_esel": [33, 16 * 128], "c_qaug": [4, 8 * cfg.T], "c_kaug": [4, cfg.T],
    "c_kaugc": [4, 128],
}

WEIGHT_SHAPES = lambda L: {
    "norm_mix": [L, D], "norm_ffn": [L, D], "w_in": [L, D, IN_W], "nsa_w_cmp_k": [L, 2048, 64],
    "nsa_w_cmp_v": [L, 2048, 64], "nsa_cmp_pe": [L, 32, 64], "hgrn_norm": [L, 512],
    "hgrn_lower_bounds": [L, 512], "w_branch_sb": [L, 512, D], "w_branch_nsa": [L, 512, D],
    "w_branch_hgrn": [L, 512, D], "w_out": [L, D, D], "peer_w_q": [L, D, D],
    "peer_sub_keys": [L, 2, 128, 64], "peer_u": [L, 16384, D], "peer_v": [L, 16384, D], "norm_final": [D],
}


def build(cfg, stages=None, debug=()):
    nc = bass.Bass("TRN2", target_bir_lowering=False)
    T, NB, L, N, NT = cfg.T, cfg.NB, cfg.L, cfg.N, cfg.NT
    NG = N // 512
    NQT = T // 512
    NKB = T // 128
    NCMP, NBLK, NCH = cfg.NCMP, cfg.NBLK, cfg.NCH

    def din(name, shape, dt=F32):
        return nc.dram_tensor(name, list(shape), dt, kind="ExternalInput").ap()

    x_in = din("x", [N, D])
    W = {k: din(k, s) for k, s in WEIGHT_SHAPES(L).items()}
    CD = {k: din(k, s) for k, s in CONST_SHAPES(cfg).items()}
    xres = nc.dram_tensor("out", [N, D], F32, kind="ExternalOutput").ap()

    def scr(name, shape, dt):
        kind = "ExternalOutput" if name in debug else "Internal"
        return nc.dram_tensor(name, list(shape), dt, kind=kind).ap()

    S = {
        "sbqT": scr("sbqT", [512, N], BF16), "sbkT": scr("sbkT", [512, N], BF16), "sbv": scr("sbv", [N, 512], BF16),
        "nqT": scr("nqT", [512, N], BF16), "kcmpT": scr("kcmpT", [128, N], BF16), "vcmpT": scr("vcmpT", [128, N], BF16),
        "kselT": scr("kselT", [128, N], BF16), "vsel": scr("vsel", [N, 128], BF16), "kwinT": scr("kwinT", [128, N], BF16),
        "vwin": scr("vwin", [N, 128], BF16), "ngT": scr("ngT", [24, N], F32), "hqT": scr("hqT", [512, N], F32),
        "hfT": scr("hfT", [512, N], F32), "hi": scr("hi", [N, 512], BF16), "hgT": scr("hgT", [512, N], BF16),
        "gmT": scr("gmT", [3072, N], BF16),
        "sboT": scr("sboT", [512, N], BF16), "nsaoT": scr("nsaoT", [512, N], BF16), "hgoT": scr("hgoT", [512, N], BF16),
    }
    S["uvbf"] = scr("uvbf", [16384, 2 * D], BF16)
    if "xdbg" in debug:
        S["xdbg"] = scr("xdbg", [N, D], F32)
    if "pdbg" in debug:
        S["pdbg"] = scr("pdbg", [N, 4, 128], F32)

    DR = Res

    with ExitStack() as top:
        c = Ctx(nc, top)
        rr = [0]
        REG0 = nc.gpsimd.to_reg(0.0)
        REGF = nc.gpsimd.to_reg(FORCE)
        REGN = nc.gpsimd.to_reg(NEG)

        def alt(engs=("dve", "act")):
            rr[0] += 1
            return engs[rr[0] % len(engs)]

        def copy_op(eng, out, in_, reads, writes):
            if eng == "act":
                c.op("act", lambda e: e.activation(out, in_, AF.Copy), reads, writes)
            else:
                c.op(eng, lambda e: e.tensor_copy(out, in_), reads, writes)

        def load_const(name, shape, src):
            t, r = c.sb(top, name, shape, F32)
            c.dma("sp", t[:], src, [], [r])
            return t, r

        identf, identfr = load_const("identf", [128, 128], CD["c_ident"])
        ustrf, ustrfr = load_const("ustrf", [128, 128], CD["c_ustr"])
        overlap, overlapr = load_const("overlap", [128, 32], CD["c_overlap"])
        trimask, trimaskr = load_const("trimask", [64, 512], CD["c_trimask"])
        egate, egater = load_const("egate", [24, 24 * 128], CD["c_egate"])
        bfspecs = [("identb", [128, 128], "c_ident"), ("esel", [33, 16 * 128], "c_esel"), ("qaug", [4, 8 * T], "c_qaug"),
                   ("kaug", [4, T], "c_kaug"), ("kaugc", [4, 128], "c_kaugc")]
        bft = {}
        for name, shape, cn in bfspecs:
            bft[name] = c.sb(top, name, shape, BF16)
        identb, identbr = bft["identb"]
        esel, eselr = bft["esel"]
        qaug, qaugr = bft["qaug"]
        kaug, kaugr = bft["kaug"]
        kaugc, kaugcr = bft["kaugc"]
        onesf, onesfr = c.sb(top, "onesf", [128, 128], F32)
        onesb, onesbr = c.sb(top, "onesb", [128, 128], BF16)
        zerob, zerobr = c.sb(top, "zerob", [128, 512], BF16)
        c.op("pool", lambda e: e.memset(onesf[:], 1.0), [], [onesfr])
        c.op("pool", lambda e: e.memset(onesb[:], 1.0), [], [onesbr])
        c.op("pool", lambda e: e.memset(zerob[:], 0.0), [], [zerobr])
        lower, lowerr = c.sb(top, "lower", [128, 4, L], F32)
        oml, omlr = c.sb(top, "oml", [128, 4, L], F32)
        with ExitStack() as tmp:
            for name, shape, cn in bfspecs:
                stg, stgr = c.sb(tmp, "stg_" + name, shape, F32)
                c.dma("sp", stg[:], CD[cn], [], [stgr])
                c.op("pool", lambda e: e.tensor_copy(bft[name][0][:], stg[:]), [stgr], [bft[name][1]])
            c.barrier()
        with ExitStack() as tmp:
            lbr, lbrr = c.sb(tmp, "lbr", [128, 4, L], F32)
            lsum, lsumr = c.sb(tmp, "lsum", [128, 4], F32)
            with nc.allow_non_contiguous_dma(reason="tiny param layout"):
                for hh in range(4):
                    c.dma("sp", lbr[:, hh, :], W["hgrn_lower_bounds"][:, hh * 128:(hh + 1) * 128].rearrange("l k -> k l"), [], [lbrr])
            c.op("act", lambda e: e.activation(lbr[:], lbr[:], AF.Exp), [lbrr], [lbrr])
            c.op("dve", lambda e: e.tensor_reduce(lsum[:], lbr[:], AX.X, ALU.add), [lbrr], [lsumr])
            c.op("dve", lambda e: e.reciprocal(lsum[:], lsum[:]), [lsumr], [lsumr])
            c.op("dve", lambda e: e.tensor_tensor(lbr[:], lbr[:], lsum[:].unsqueeze(2).to_broadcast([128, 4, L]), ALU.mult), [lbrr, lsumr], [lbrr])
            c.op("dve", lambda e: e.memset(lower[:], 0.0), [], [lowerr])
            for l in range(1, L):
                c.op("dve", lambda e: e.tensor_tensor(lower[:, :, l], lower[:, :, l - 1], lbr[:, :, l], ALU.add), [lowerr, lbrr], [lowerr])
            c.op("dve", lambda e: e.tensor_scalar(oml[:], lower[:], -1.0, 1.0, ALU.mult, ALU.add), [lowerr], [omlr])
            c.barrier()
        c.dma("sp", xres, x_in, [], [DR()])
        c.barrier()

        def rms_rstd(ssum, ssr, rstd, rstdr, n):
            c.op("act", lambda e: e.activation(rstd, ssum, AF.Ln, bias=EPS, scale=1.0 / n), [ssr], [rstdr])
            c.op("act", lambda e: e.activation(rstd, rstd, AF.Exp, scale=-0.5), [rstdr], [rstdr])

        def stage_proj(l):
            with ExitStack() as st:
                hT, _ = c.sb(st, "hT", [128, 8, N], BF16)
                hTr = [Res() for _ in range(NG)]
                xt = [c.sb(st, "xt%d" % i, [128, D], F32) for i in range(2)]
                hb = [c.sb(st, "hb%d" % i, [128, D], BF16) for i in range(2)]
                junk, junkr = c.sb(st, "junk", [128, D], F32)
                ss = [c.sb(st, "ss%d" % i, [128, 1], F32) for i in range(2)]
                rs = [c.sb(st, "rs%d" % i, [128, 1], F32) for i in range(2)]
                pT = [c.ps(st, "pT%d" % i, [128, 8, 128], BF16) for i in range(2)]
                gcol, gcolr = c.sb(st, "gcol", [128, 8], F32)
                with nc.allow_non_contiguous_dma(reason="tiny param layout"):
                    c.dma("sp", gcol[:], W["norm_mix"][l].rearrange("(c p) -> p c", p=128), [], [gcolr])
                for tt in range(NT):
                    i = tt % 2
                    (x_, xr), (h_, hr), (s_, sr), (r_, rr_), (p_, pr) = xt[i], hb[i], ss[i], rs[i], pT[i]
                    c.dma("sp", x_[:], xres[tt * 128:(tt + 1) * 128, :], [], [xr])
                    c.op("act", lambda e: e.activation(junk[:], x_[:], AF.Square, accum_out=s_[:]), [xr], [junkr, sr])
                    rms_rstd(s_[:], sr, r_[:], rr_, D)
                    c.op("dve", lambda e: e.tensor_scalar(h_[:], x_[:], r_[:], None, ALU.mult), [xr, rr_], [hr])
                    for k in range(8):
                        c.op("pe", lambda e: e.transpose(p_[:, k, :], h_[:, k * 128:(k + 1) * 128], identb[:]), [hr, identbr], [pr])
                    copy_op(alt(), hT[:, :, tt * 128:(tt + 1) * 128], p_[:], [pr], [hTr[tt // 4]])

                wst = [c.sb(st, "wst%d" % i, [128, 8, 512], F32) for i in range(2)]
                wbf = [c.sb(st, "wbf%d" % i, [128, 8, 512], BF16) for i in range(2)]
                pb = [c.ps(st, "pb%d" % i, [128, 512], F32) for i in range(4)]
                ob16 = [c.sb(st, "ob16_%d" % i, [128, 512], BF16) for i in range(3)]
                ob32 = [c.sb(st, "ob32_%d" % i, [128, 512], F32) for i in range(2)]
                offs = np.cumsum((0,) + IN_SPLITS)
                jobs = [
                    (offs[0], 512, "fm", S["sbqT"], 0, BF16, None), (offs[1], 512, "fm", S["sbkT"], 0, BF16, None),
                    (offs[2], 512, "tm", S["sbv"], 0, BF16, None), (offs[3], 512, "fm", S["nqT"], 0, BF16, None),
                    (offs[4] + 0, 128, "fm", S["kcmpT"], 0, BF16, None), (offs[4] + 128, 128, "fm", S["vcmpT"], 0, BF16, None),
                    (offs[4] + 256, 128, "fm", S["kselT"], 0, BF16, None), (offs[4] + 384, 128, "tm", S["vsel"], 0, BF16, None),
                    (offs[4] + 512, 128, "fm", S["kwinT"], 0, BF16, None), (offs[4] + 640, 128, "tm", S["vwin"], 0, BF16, None),
                    (offs[5], 24, "fm", S["ngT"], 0, F32, None), (offs[6], 512, "fm", S["hqT"], 0, F32, AF.Silu),
                    (offs[7], 512, "fm", S["hfT"], 0, F32, AF.Sigmoid), (offs[8], 512, "tm", S["hi"], 0, BF16, None),
                    (offs[9], 512, "fm", S["hgT"], 0, BF16, AF.Silu),
                ] + [(offs[10] + 512 * j, 512, "fm", S["gmT"], 512 * j, BF16, AF.Sigmoid) for j in range(6)]
                cnt = [0, 0, 0]
                for ji, (col0, ncols, kind, dst, doff, dt, func) in enumerate(jobs):
                    col0 = int(col0)
                    (ws, wsr), (wb_, wbr) = wst[ji % 2], wbf[ji % 2]
                    c.dma("sp", ws[:, :, :ncols], W["w_in"][l][:, col0:col0 + ncols].rearrange("(c p) n -> p c n", p=128), [], [wsr])
                    c.op(alt(("dve", "pool")), lambda e: e.tensor_tensor(wb_[:, :, :ncols], ws[:, :, :ncols], gcol[:].unsqueeze(2).to_broadcast([128, 8, ncols]), ALU.mult), [wsr, gcolr], [wbr])

                    def evac(pp, ppr, rows, cols):
                        if dt == BF16:
                            o_, or_ = ob16[cnt[0] % 3]
                            cnt[0] += 1
                        else:
                            o_, or_ = ob32[cnt[1] % 2]
                            cnt[1] += 1
                        if func is not None:
                            c.op("act", lambda e: e.activation(o_[:rows, :cols], pp[:rows, :cols], func), [ppr], [or_])
                        else:
                            copy_op(alt(), o_[:rows, :cols], pp[:rows, :cols], [ppr], [or_])
                        return o_, or_

                    if kind == "tm":
                        for tt in range(NT):
                            pp, ppr = pb[cnt[2] % 4]
                            cnt[2] += 1
                            for k in range(8):
                                c.op("pe", lambda e: e.matmul(pp[:, :ncols], hT[:, k, tt * 128:(tt + 1) * 128], wb_[:, k, :ncols], start=(k == 0), stop=(k == 7)), [hTr[tt // 4], wbr], [ppr])
                            o_, or_ = evac(pp, ppr, 128, ncols)
                            c.dma("sp", dst[tt * 128:(tt + 1) * 128, doff:doff + ncols], o_[:, :ncols], [or_], [DR()])
                    else:
                        for f0 in range(0, ncols, 128):
                            F = min(128, ncols - f0)
                            for tg in range(NG):
                                pp, ppr = pb[cnt[2] % 4]
                                cnt[2] += 1
                                for k in range(8):
                                    c.op("pe", lambda e: e.matmul(pp[:F, :], wb_[:, k, f0:f0 + F], hT[:, k, tg * 512:(tg + 1) * 512], start=(k == 0), stop=(k == 7)), [hTr[tg], wbr], [ppr])
                                o_, or_ = evac(pp, ppr, F, 512)
                                c.dma("sp", dst[doff + f0:doff + f0 + F, tg * 512:(tg + 1) * 512], o_[:F, :], [or_], [DR()])
                c.barrier()

        def stage_sb(l):
            with ExitStack() as st:
                qT, qTr = c.sb(st, "qT", [128, 4, T], BF16)
                kT, kTr = c.sb(st, "kT", [128, 4, T], BF16)
                v, vr = c.sb(st, "v", [128, NKB, 512], BF16)
                ex = [c.sb(st, "ex%d" % i, [128, 512], F32) for i in range(3)]
                spl = [c.sb(st, "spl%d" % i, [128, 512], F32) for i in range(3)]
                t1 = [c.sb(st, "t1%d" % i, [128, 512], F32) for i in range(3)]
                at = [c.sb(st, "at%d" % i, [128, 512], BF16) for i in range(3)]
                M, Mr = c.sb(st, "M", [128, 512], F32)
                ustr_r, ustr_rr = c.sb(st, "ustr_r", [128, 128], F32)
                ones_r, ones_rr = c.sb(st, "ones_r", [128, 128], F32)
                c.op("dve", lambda e: e.tensor_copy(ustr_r[:].bitcast(F32R), ustrf[:]), [ustrfr], [ustr_rr])
                c.op("dve", lambda e: e.tensor_copy(ones_r[:].bitcast(F32R), onesf[:]), [onesfr], [ones_rr])
                osb = [c.sb(st, "osb%d" % i, [128, 512], BF16) for i in range(2)]
                pz = [c.ps(st, "pz%d" % i, [128, 512], F32) for i in range(3)]
                pcum = [c.ps(st, "pcum%d" % i, [128, 512], F32) for i in range(3)]
                pacc = [c.ps(st, "pacc%d" % i, [128, 512], F32) for i in range(2)]
                it = 0
                ia = 0
                for b in range(NB):
                    bs = slice(b * T, (b + 1) * T)
                    c.dma("sp", qT[:], S["sbqT"][:, bs].rearrange("(c p) t -> p c t", p=128), [], [qTr])
                    c.dma("sp", kT[:], S["sbkT"][:, bs].rearrange("(c p) t -> p c t", p=128), [], [kTr])
                    c.dma("sp", v[:], S["sbv"][bs, :].rearrange("(k p) f -> p k f", p=128), [], [vr])
                    for h in range(8):
                        ch, pb_ = h // 2, (h % 2) * 64
                        for qt in range(NQT):
                            t0 = qt * 512
                            nkb = (t0 + 512) // 128
                            acc, accr = pacc[ia % 2]
                            o_, or_ = osb[ia % 2]
                            ia += 1
                            c.op("pe", lambda e: e.matmul(acc[:, :], zerob[:, :128], zerob[:, :512], start=True, stop=False), [zerobr], [accr])
                            c.op("pool", lambda e: e.tensor_copy(M[:].bitcast(F32R), zerob[:]), [zerobr], [Mr])
                            for idx, kb in enumerate(reversed(range(nkb))):
                                s0 = kb * 128
                                tlo = max(0, s0 - t0)
                                diag = s0 >= t0
                                cs = slice(tlo, 512)
                                (z, zr), (cu, cur) = pz[it % 3], pcum[it % 3]
                                (e_, er), (sp_, spr), (t_, tr_), (a_, ar_) = ex[it % 3], spl[it % 3], t1[it % 3], at[it % 3]
                                it += 1
                                c.op("pe", lambda e: e.matmul(z[:, cs], kT[pb_:pb_ + 64, ch, s0:s0 + 128], qT[pb_:pb_ + 64, ch, t0 + tlo:t0 + 512], start=True, stop=True), [kTr, qTr], [zr])
                                c.op("act", lambda e: e.activation(e_[:, cs], z[:, cs], AF.Exp, scale=0.125), [zr], [er])
                                c.op("act", lambda e: e.activation(sp_[:, cs].bitcast(F32R), e_[:, cs], AF.Ln, bias=1.0), [er], [spr])
                                if diag:
                                    c.op("pool", lambda e: e.affine_select(sp_[:, tlo:tlo + 128].bitcast(F32R), sp_[:, tlo:tlo + 128], [[1, 128]], ALU.is_gt, REG0, base=0, channel_multiplier=-1), [spr], [spr])
                                c.op("pe", lambda e: e.matmul(cu[:, cs], ustr_r[:].bitcast(F32R), sp_[:, cs].bitcast(F32R), start=True, stop=(idx == 0)), [ustr_rr, spr], [cur])
                                if idx > 0:
                                    c.op("pe", lambda e: e.matmul(cu[:, cs], ones_r[:].bitcast(F32R), M[:, cs].bitcast(F32R), start=False, stop=True), [ones_rr, Mr], [cur])
                                c.op("dve", lambda e: e.scalar_tensor_tensor(t_[:, cs], z[:, cs], 0.125, sp_[:, cs], ALU.mult, ALU.subtract), [zr, spr], [tr_])
                                c.op("dve", lambda e: e.tensor_tensor(t_[:, cs], t_[:, cs], cu[:, cs], ALU.subtract), [tr_, cur], [tr_])
                                c.op("act", lambda e: e.activation(a_[:, cs], t_[:, cs], AF.Exp), [tr_], [ar_])
                                if diag:
                                    c.op("pool", lambda e: e.affine_select(a_[:, tlo:tlo + 128], a_[:, tlo:tlo + 128], [[1, 128]], ALU.is_gt, REG0, base=0, channel_multiplier=-1), [ar_], [ar_])
                                if kb > 0:
                                    c.op("pool", lambda e: e.tensor_tensor(M[:, cs].bitcast(F32R), M[:, cs], sp_[:, cs], ALU.add), [Mr, spr], [Mr])
                                c.op("pe", lambda e: e.matmul(acc[:, cs], v[:, kb, ch * 128:(ch + 1) * 128], a_[:, cs], start=False, stop=(kb == 0)), [vr, ar_], [accr])
                            copy_op(alt(), o_[pb_:pb_ + 64, :], acc[pb_:pb_ + 64, :], [accr], [or_])
                            c.dma("sp", S["sboT"][h * 64:(h + 1) * 64, b * T + t0:b * T + t0 + 512], o_[pb_:pb_ + 64, :], [or_], [DR()])
                    c.barrier()

        def stage_nsa(l):
            with ExitStack() as st:
                qT, qTr = c.sb(st, "nqT", [128, 4, T], BF16)
                ksel2 = [c.sb(st, "ksel2_%d" % g, [128, T], BF16) for g in range(2)]
                kwin2 = [c.sb(st, "kwin2_%d" % g, [128, T], BF16) for g in range(2)]
                vsel2, vsel2r = c.sb(st, "vsel2", [128, NKB, 2, 128], BF16)
                vwin2, vwin2r = c.sb(st, "vwin2", [128, NKB, 2, 128], BF16)
                gT, gTr = c.sb(st, "gT", [24, T], F32)
                nacc, _ = c.sb(st, "nacc", [128, 4, T], F32)
                naccr = [[Res() for _ in range(NQT)] for _ in range(8)]
                selT = [c.sb(st, "selT%d" % g, [33, T], BF16) for g in range(2)]
                selTr = [[Res() for _ in range(NQT)] for _ in range(2)]
                kc2 = [[c.sb(st, "kc2_%d_%d" % (b, g), [128, 128], BF16) for g in range(2)] for b in range(NB)]
                vc2 = [[c.sb(st, "vc2_%d_%d" % (b, g), [128, 128], F32) for g in range(2)] for b in range(NB)]
                with ExitStack() as tmp:
                    wk2, wk2r = c.sb(tmp, "wk2", [64, 32, 128], BF16)
                    wv2, wv2r = c.sb(tmp, "wv2", [64, 32, 128], BF16)
                    peT, peTr = c.sb(tmp, "peT", [64, 32], BF16)
                    wstg, wstgr = c.sb(tmp, "wstg", [64, 32, 64], F32)
                    pstg, pstgr = c.sb(tmp, "pstg", [64, 32], F32)
                    xcT = [c.sb(tmp, "xcT%d" % i, [64, T], BF16) for i in range(2)]
                    cb, cbr = c.sb(tmp, "cb", [128, 1], F32)
                    vcT, vcTr = c.sb(tmp, "vcT", [128, 128], F32)
                    pxc, pxcr = c.ps(tmp, "pxc", [128, 512], F32)
                    for (wsrc, wdst, wdr) in ((W["nsa_w_cmp_k"][l], wk2, wk2r), (W["nsa_w_cmp_v"][l], wv2, wv2r)):
                        c.dma("sp", wstg[:], wsrc.rearrange("(p d) e -> d p e", d=64), [], [wstgr])
                        c.op("dve", lambda e: e.tensor_copy(wdst[:, :, 0:64], wstg[:]), [wstgr], [wdr])
                        c.op("pool", lambda e: e.tensor_copy(wdst[:, :, 64:128], wstg[:]), [wstgr], [wdr])
                    with nc.allow_non_contiguous_dma(reason="tiny param layout"):
                        c.dma("sp", pstg[:], W["nsa_cmp_pe"][l].rearrange("p d -> d p"), [], [pstgr])
                    c.op("dve", lambda e: e.tensor_copy(peT[:], pstg[:]), [pstgr], [peTr])
                    for b in range(NB):
                        bs = slice(b * T, (b + 1) * T)
                        for g in range(2):
                            for wi, (src, w2, w2r) in enumerate(((S["kcmpT"], wk2, wk2r), (S["vcmpT"], wv2, wv2r))):
                                xc, xcr = xcT[wi]
                                c.dma("sp", xc[:], src[g * 64:(g + 1) * 64, bs], [], [xcr])
                                for p in range(32):
                                    c.op("pe", lambda e: e.matmul(pxc[:, 0:NCMP], w2[:, p, :], xc[:, p:p + 16 * (NCMP - 1) + 1:16], start=(p == 0), stop=(p == 31)), [w2r, xcr], [pxcr])
                                for p in range(32):
                                    c.op("pe", lambda e: e.matmul(pxc[:, 256:257], w2[:, p, :], peT[:, p:p + 1], start=(p == 0), stop=(p == 31)), [w2r, peTr], [pxcr])
                                c.op("dve", lambda e: e.tensor_copy(cb[:], pxc[:, 256:257]), [pxcr], [cbr])
                                if wi == 0:
                                    c.op("dve", lambda e: e.tensor_scalar(kc2[b][g][0][:, 0:NCMP], pxc[:, 0:NCMP], cb[:], None, ALU.add), [pxcr, cbr], [kc2[b][g][1]])
                                else:
                                    c.op("dve", lambda e: e.tensor_scalar(vcT[:, 0:NCMP], pxc[:, 0:NCMP], cb[:], None, ALU.add), [pxcr, cbr], [vcTr])
                                    c.op("pe", lambda e: e.transpose(pxc[0:NCMP, 0:128], vcT[:, 0:NCMP], identf[:]), [vcTr, identfr], [pxcr])
                                    c.op("act", lambda e: e.activation(vc2[b][g][0][0:NCMP, :], pxc[0:NCMP, 0:128], AF.Copy), [pxcr], [vc2[b][g][1]])
                    c.barrier()
                pcx = [c.sb(st, "pcx%d" % i, [128, 512], F32) for i in range(2)]
                pcn = [c.sb(st, "pcn%d" % i, [128, 512], F32) for i in range(1)] * 2
                rsb = [c.sb(st, "rsb%d" % i, [128, 512], F32) for i in range(2)]
                eg = [c.sb(st, "eg%d" % i, [128, 512], F32) for i in range(2)]
                wg = [c.sb(st, "wg%d" % i, [128, 512], F32) for i in range(2)]
                ctr = [c.sb(st, "ctr%d" % i, [128, 512], F32) for i in range(1)] * 2
                impT, impTr = c.sb(st, "impT", [32, 512], F32)
                imp, impr = c.sb(st, "imp", [128, 4, 32], F32)
                m8, m8r = c.sb(st, "m8", [128, 4, 8], F32)
                sel, selr = c.sb(st, "sel", [128, 4, 32], F32)
                pw = [c.sb(st, "pw%d" % i, [128, 512], BF16) for i in range(5)]
                pwm = [c.sb(st, "pwm%d" % i, [128, 512], BF16) for i in range(2)]
                osb = [c.sb(st, "nosb%d" % i, [128, 512], BF16) for i in range(2)]
                pz = [c.ps(st, "npz%d" % i, [128, 512], F32) for i in range(2)]
                pm, pmr = c.ps(st, "npm", [128, 512], F32)
                pacc, paccr = c.ps(st, "npacc", [128, 512], F32)
                prs, prsr = c.ps(st, "nprs", [128, 512], F32)
                pg, pgr = c.ps(st, "npg", [128, 512], F32)
                pimp, pimpr = c.ps(st, "npimp", [128, 512], F32)
                pxx, pxxr = c.ps(st, "npxx", [128, 512], F32)
                it = 0
                io = 0
                for g in range(2):
                    c.op("pool", lambda e: e.memset(selT[g][0][:], 0.0), [], [selTr[g][0]])
                    c.op("pool", lambda e: e.memset(selT[g][0][32:33, :], 1.0), [selTr[g][0]], [selTr[g][0]])
                c.barrier()
                zb5 = None
                for b in range(NB):
                    bs = slice(b * T, (b + 1) * T)
                    c.dma("sp", qT[:], S["nqT"][:, bs].rearrange("(c p) t -> p c t", p=128), [], [qTr])
                    for g in range(2):
                        for half in range(2):
                            c.dma("sp", ksel2[g][0][half * 64:(half + 1) * 64, :], S["kselT"][g * 64:(g + 1) * 64, bs], [], [ksel2[g][1]])
                            c.dma("sp", kwin2[g][0][half * 64:(half + 1) * 64, :], S["kwinT"][g * 64:(g + 1) * 64, bs], [], [kwin2[g][1]])
                            c.dma("sp", vsel2[:, :, g, half * 64:(half + 1) * 64], S["vsel"][bs, g * 64:(g + 1) * 64].rearrange("(k p) f -> p k f", p=128), [], [vsel2r])
                            c.dma("sp", vwin2[:, :, g, half * 64:(half + 1) * 64], S["vwin"][bs, g * 64:(g + 1) * 64].rearrange("(k p) f -> p k f", p=128), [], [vwin2r])
                    c.dma("sp", gT[:], S["ngT"][:, bs], [], [gTr])
                    for g in range(2):
                        for qt in range(NQT):
                            t0 = qt * 512
                            ts = slice(t0, t0 + 512)
                            for hp in range(4):
                                h = g * 4 + hp
                                ch, pb_ = h // 2, (h % 2) * 64
                                (z, zr) = pz[it % 2]
                                (px, pxr), (pn, pnr), (rb, rbr), (eg_, egr), (w_, wr_) = pcx[it % 2], pcn[it % 2], rsb[it % 2], eg[it % 2], wg[it % 2]
                                it += 1
                                c.op("pe", lambda e: e.matmul(z[0:NCMP, :], kc2[b][g][0][pb_:pb_ + 64, 0:NCMP], qT[pb_:pb_ + 64, ch, ts], start=True, stop=False), [kc2[b][g][1], qTr], [zr])
                                c.op("pe", lambda e: e.matmul(z[0:NCMP, :], kaugc[0:4, 0:NCMP], qaug[0:4, h * T + t0:h * T + t0 + 512], start=False, stop=True), [kaugcr, qaugr], [zr])
                                c.op("act", lambda e: e.activation(px[0:NCMP, :], z[0:NCMP, :], AF.Exp, scale=0.125), [zr], [pxr])
                                c.op("pool", lambda e: e.affine_select(px[0:NCMP, :], px[0:NCMP, :], [[1, 512]], ALU.is_ge, REG0, base=t0 - 31, channel_multiplier=-16), [pxr], [pxr])
                                c.op("pe", lambda e: e.matmul(prs[:, :], onesf[0:NCMP, :], px[0:NCMP, :], start=True, stop=True), [onesfr, pxr], [prsr])
                                c.op("pe", lambda e: e.matmul(pacc[:, :], vc2[b][g][0][0:NCMP, :], px[0:NCMP, :], start=True, stop=True), [vc2[b][g][1], pxr], [paccr])
                                c.op("pe", lambda e: e.matmul(pg[:, :], egate[0:24, (h * 3 + 0) * 128:(h * 3 + 1) * 128], gT[0:24, ts], start=True, stop=True), [egater, gTr], [pgr])
                                c.op("dve", lambda e: e.tensor_scalar(rb[:], prs[:], 1e-30, None, ALU.add), [prsr], [rbr])
                                c.op("dve", lambda e: e.reciprocal(rb[:], rb[:]), [rbr], [rbr])
                                c.op("dve", lambda e: e.tensor_tensor(pn[0:NCMP, :], px[0:NCMP, :], rb[0:NCMP, :], ALU.mult), [pxr, rbr], [pnr])
                                c.op("pe", lambda e: e.matmul(pimp[0:NBLK, :], overlap[0:NCMP, 0:NBLK], pn[0:NCMP, :], start=(hp == 0), stop=(hp == 3)), [overlapr, pnr], [pimpr])
                                c.op("act", lambda e: e.activation(eg_[:], pg[:], AF.Exp, scale=-1.0), [pgr], [egr])
                                c.op("dve", lambda e: e.tensor_scalar(eg_[:], eg_[:], 1.0, None, ALU.add), [egr], [egr])
                                c.op("dve", lambda e: e.reciprocal(eg_[:], eg_[:]), [egr], [egr])
                                c.op("dve", lambda e: e.tensor_tensor(w_[:], eg_[:], rb[:], ALU.mult), [egr, rbr], [wr_])
                                c.op("dve", lambda e: e.tensor_tensor(nacc[pb_:pb_ + 64, ch, ts], pacc[pb_:pb_ + 64, :], w_[pb_:pb_ + 64, :], ALU.mult), [paccr, wr_], [naccr[h][qt]])
                            c.op("act", lambda e: e.activation(impT[0:NBLK, :], pimp[0:NBLK, :], AF.Copy), [pimpr], [impTr])
                            for j in range(4):
                                c.op("pe", lambda e: e.transpose(pxx[:, j * 32:j * 32 + NBLK], impT[0:NBLK, j * 128:(j + 1) * 128], identf[0:NBLK, 0:NBLK]), [impTr, identfr], [pxxr])
                            c.op("act", lambda e: e.activation(imp[:, :, 0:NBLK], pxx[:, 0:128].rearrange("p (j n) -> p j n", n=32)[:, :, 0:NBLK], AF.Copy), [pxxr], [impr])
                            c.op("pool", lambda e: e.affine_select(imp[:, :, 0:NBLK], imp[:, :, 0:NBLK], [[128, 4], [-64, NBLK]], ALU.is_ge, REGF, base=t0 - 64, channel_multiplier=1), [impr], [impr])
                            c.op("pool", lambda e: e.affine_select(imp[:, :, 0:NBLK], imp[:, :, 0:NBLK], [[128, 4], [-64, NBLK]], ALU.is_ge, REGN, base=t0, channel_multiplier=1), [impr], [impr])
                            c.op("pool", lambda e: e.memset(imp[:, :, 0:1], FORCE), [impr], [impr])
                            for j in range(4):
                                c.op("dve", lambda e: e.max(m8[:, j, :], imp[:, j, 0:NBLK]), [impr], [m8r])
                            for j in range(4):
                                c.op("dve", lambda e: e.tensor_scalar(sel[:, j, 0:NBLK], imp[:, j, 0:NBLK], m8[:, j, 3:4], None, ALU.is_ge), [impr, m8r], [selr])
                            for j in range(4):
                                c.op("pe", lambda e: e.transpose(pxx[0:NBLK, j * 128:(j + 1) * 128], sel[:, j, 0:NBLK], identf[:]), [selr, identfr], [pxxr])
                            c.op("act", lambda e: e.activation(selT[g][0][0:NBLK, ts], pxx[0:NBLK, :], AF.Copy), [pxxr], [selTr[g][qt]])
                    for h in range(8):
                        g = h // 4
                        ch, pb_ = h // 2, (h % 2) * 64
                        for qt in range(NQT):
                            t0 = qt * 512
                            ts = slice(t0, t0 + 512)
                            for br in (2, 1):
                                c.op("pe", lambda e: e.matmul(pacc[:, :], zerob[:, :128], zerob[:, :512], start=True, stop=False), [zerobr], [paccr])
                                c.op("pe", lambda e: e.matmul(prs[:, :], zerob[:, :128], zerob[:, :512], start=True, stop=False), [zerobr], [prsr])
                                if br == 2:
                                    kbs = list(range(max(0, t0 // 128 - 2), t0 // 128 + 4))
                                    k2, k2r = kwin2[g]
                                    v2, v2r = vwin2, vwin2r
                                else:
                                    kbs = list(range(0, (t0 + 512) // 128))
                                    k2, k2r = ksel2[g]
                                    v2, v2r = vsel2, vsel2r
                                for ki, kb in enumerate(kbs):
                                    s0 = kb * 128
                                    tlo = max(0, s0 - t0)
                                    thi = min(512, s0 - t0 + 383) if br == 2 else 512
                                    cs = slice(tlo, thi)
                                    last = ki == len(kbs) - 1
                                    zb5 = [pz[0], pz[1], (pimp, pimpr), (pxx, pxxr), (pm, pmr)]
                                    (z, zr) = zb5[it % 5]
                                    (p_, pr_) = pw[it % 5]
                                    it += 1
                                    c.op("pe", lambda e: e.matmul(z[:, cs], k2[pb_:pb_ + 64, s0:s0 + 128], qT[pb_:pb_ + 64, ch, t0 + tlo:t0 + thi], start=True, stop=False), [k2r, qTr], [zr])
                                    c.op("pe", lambda e: e.matmul(z[:, cs], kaug[0:4, s0:s0 + 128], qaug[0:4, h * T + t0 + tlo:h * T + t0 + thi], start=False, stop=(br == 2)), [kaugr, qaugr], [zr])
                                    if br == 1:
                                        c.op("pe", lambda e: e.matmul(z[:, cs], esel[0:33, kb * 128:(kb + 1) * 128], selT[g][0][0:33, t0 + tlo:t0 + 512], start=False, stop=True), [eselr, selTr[g][qt]], [zr])
                                    c.op("act", lambda e: e.activation(p_[:, cs], z[:, cs], AF.Exp, scale=0.125), [zr], [pr_])
                                    if s0 >= t0:
                                        c.op("pool", lambda e: e.affine_select(p_[:, tlo:tlo + 128], p_[:, tlo:tlo + 128], [[1, 128]], ALU.is_ge, REG0, base=0, channel_multiplier=-1), [pr_], [pr_])
                                    if br == 2:
                                        r0 = s0 - t0 + 255
                                        a0, a1 = max(r0, 0), min(r0 + 128, 512)
                                        if a1 > a0:
                                            xs = a0 - r0
                                            c.op("pool", lambda e: e.affine_select(p_[:, a0:a1], p_[:, a0:a1], [[-1, a1 - a0]], ALU.is_ge, REG0, base=-xs, channel_multiplier=1), [pr_], [pr_])
                                    src, srcr = p_, pr_
                                    c.op("pe", lambda e: e.matmul(pacc[:, cs], v2[:, kb, g, :], src[:, cs], start=False, stop=last), [v2r, srcr], [paccr])
                                    c.op("pe", lambda e: e.matmul(prs[:, cs], onesb[:], src[:, cs], start=False, stop=last), [onesbr, srcr], [prsr])
                                (eg_, egr), (w_, wr_), (ct, ctr_) = eg[io % 2], wg[io % 2], ctr[io % 2]
                                io += 1
                                c.op("pe", lambda e: e.matmul(pg[:, :], egate[0:24, (h * 3 + br) * 128:(h * 3 + br + 1) * 128], gT[0:24, ts], start=True, stop=True), [egater, gTr], [pgr])
                                c.op("act", lambda e: e.activation(eg_[:], pg[:], AF.Exp, scale=-1.0), [pgr], [egr])
                                c.op("dve", lambda e: e.scalar_tensor_tensor(w_[:], eg_[:], 1.0, prs[:], ALU.add, ALU.mult), [egr, prsr], [wr_])
                                c.op("dve", lambda e: e.reciprocal(w_[:], w_[:]), [wr_], [wr_])
                                c.op("dve", lambda e: e.tensor_tensor(ct[pb_:pb_ + 64, :], pacc[pb_:pb_ + 64, :], w_[pb_:pb_ + 64, :], ALU.mult), [paccr, wr_], [ctr_])
                                c.op("pool", lambda e: e.tensor_tensor(nacc[pb_:pb_ + 64, ch, ts], nacc[pb_:pb_ + 64, ch, ts], ct[pb_:pb_ + 64, :], ALU.add), [naccr[h][qt], ctr_], [naccr[h][qt]])
                            o_, or_ = osb[io % 2]
                            c.op("act", lambda e: e.activation(o_[pb_:pb_ + 64, :], nacc[pb_:pb_ + 64, ch, ts], AF.Copy), [naccr[h][qt]], [or_])
                            c.dma("sp", S["nsaoT"][h * 64:(h + 1) * 64, b * T + t0:b * T + t0 + 512], o_[pb_:pb_ + 64, :], [or_], [DR()])
                    c.barrier()

        def stage_hgrn(l):
            with ExitStack() as st:
                ones_t, ones_tr = c.sb(st, "ones_t", [128, T], F32)
                c.op("pool", lambda e: e.memset(ones_t[:], 1.0), [], [ones_tr])
                wn, wnr = c.sb(st, "wn", [128, 4], F32)
                with nc.allow_non_contiguous_dma(reason="tiny param layout"):
                    c.dma("sp", wn[:], W["hgrn_norm"][l].rearrange("(h v) -> v h", v=128), [], [wnr])
                qh, qhr = c.sb(st, "qh", [128, T], F32)
                fs, fsr = c.sb(st, "fs", [128, T], F32)
                gl, glr = c.sb(st, "gl", [128, T], F32)
                Bc, Bcr = c.sb(st, "Bc", [128, T], F32)
                dd, ddr = c.sb(st, "dd", [128, T], F32)
                ee, eer = c.sb(st, "ee", [128, T], F32)
                qtl, qtlr = c.sb(st, "qtl", [128, T], BF16)
                ktl, ktlr = c.sb(st, "ktl", [128, T], BF16)
                qin, qinr = c.sb(st, "qin", [128, T], BF16)
                khat, khatr = c.sb(st, "khat", [128, T], BF16)
                bst, bstr = c.sb(st, "bst", [128, NCH], F32)
                eb, ebr = c.sb(st, "eb", [128, NCH], F32)
                itok, itokr = c.sb(st, "itok", [64, NCH, 128], BF16)
                sg, sgr = c.sb(st, "sg", [128, T], BF16)
                ktok, ktokr = c.sb(st, "ktok", [64, NCH, 128], BF16)
                Sm, Smr = c.sb(st, "Sm", [128, 128], F32)
                Sall, Sallr = c.sb(st, "Sall", [128, NCH, 128], BF16)
                att, attr = c.sb(st, "att", [64, NCH, 64], BF16)
                sq = [c.sb(st, "sq%d" % i, [128, 512], BF16) for i in range(2)]
                rstd = [c.sb(st, "hrstd%d" % i, [128, 512], F32) for i in range(2)]
                y1 = [c.sb(st, "y1_%d" % i, [128, 512], F32) for i in range(2)]
                yo = [c.sb(st, "yo%d" % i, [128, 512], BF16) for i in range(2)]
                ptr = [c.ps(st, "hptr%d" % i, [64, 8, 128], BF16) for i in range(2)]
                pds = [c.ps(st, "hpds%d" % i, [128, 4, 128], F32) for i in range(2)]
                pat = [c.ps(st, "hpat%d" % i, [64, 8, 64], F32) for i in range(1)]
                po = [c.ps(st, "hpo%d" % i, [128, 8, 64], F32) for i in range(2)]
                pss = [c.ps(st, "hpss%d" % i, [128, 512], F32) for i in range(1)]
                Bv = Bc[:].rearrange("p (c s) -> p c s", s=64)
                ddv = dd[:].rearrange("p (c s) -> p c s", s=64)
                for b in range(NB):
                    bs = slice(b * T, (b + 1) * T)
                    for hh in range(4):
                        fr = slice(hh * 128, (hh + 1) * 128)
                        c.dma("sp", qh[:], S["hqT"][fr, bs], [], [qhr])
                        c.dma("sp", fs[:], S["hfT"][fr, bs], [], [fsr])
                        c.dma("sp", itok[:], S["hi"][bs, fr].rearrange("(c s) v -> s c v", s=64), [], [itokr])
                        c.dma("sp", sg[:], S["hgT"][fr, bs], [], [sgr])
                        c.op("dve", lambda e: e.tensor_scalar(fs[:], fs[:], oml[:, hh, l:l + 1], lower[:, hh, l:l + 1], ALU.mult, ALU.add), [fsr, omlr, lowerr], [fsr])
                        c.op("act", lambda e: e.activation(gl[:], fs[:], AF.Ln), [fsr], [glr])
                        c.op("dve", lambda e: e.tensor_scalar(fs[:], fs[:], -1.0, 1.0, ALU.mult, ALU.add), [fsr, glr], [fsr])
                        c.op("dve", lambda e: e.tensor_tensor_scan(Bc[:], ones_t[:], gl[:], 0.0, ALU.mult, ALU.add), [ones_tr, glr], [Bcr])
                        c.op("dve", lambda e: e.tensor_tensor(ddv, Bv, Bv[:, :, 31:32].to_broadcast([128, NCH, 64]), ALU.subtract), [Bcr], [ddr])
                        c.op("act", lambda e: e.activation(ee[:], dd[:], AF.Exp), [ddr], [eer])
                        c.op("dve", lambda e: e.tensor_tensor(qtl[:], qh[:], ee[:], ALU.mult), [qhr, eer], [qtlr])
                        c.op("act", lambda e: e.activation(ee[:], dd[:], AF.Exp, scale=-1.0), [ddr, qtlr], [eer])
                        c.op("dve", lambda e: e.tensor_tensor(ktl[:], fs[:], ee[:], ALU.mult), [fsr, eer], [ktlr])
                        c.op("pool", lambda e: e.memset(bst[:, 0:1], 0.0), [], [bstr])
                        if NCH > 1:
                            c.op("pool", lambda e: e.tensor_copy(bst[:, 1:NCH], Bv[:, 0:NCH - 1, 63]), [Bcr], [bstr])
                        c.op("dve", lambda e: e.tensor_tensor(ddv, Bv, bst[:].unsqueeze(2).to_broadcast([128, NCH, 64]), ALU.subtract), [Bcr, bstr, ktlr], [ddr])
                        c.op("act", lambda e: e.activation(ee[:], dd[:], AF.Exp), [ddr, ktlr], [eer])
                        c.op("dve", lambda e: e.tensor_tensor(qin[:], qh[:], ee[:], ALU.mult), [qhr, eer], [qinr])
                        c.op("dve", lambda e: e.tensor_tensor(ddv, Bv, Bv[:, :, 63:64].to_broadcast([128, NCH, 64]), ALU.subtract), [Bcr, qinr], [ddr])
                        c.op("act", lambda e: e.activation(ee[:], dd[:], AF.Exp, scale=-1.0), [ddr, qinr], [eer])
                        c.op("dve", lambda e: e.tensor_tensor(khat[:], fs[:], ee[:], ALU.mult), [fsr, eer], [khatr])
                        c.op("dve", lambda e: e.tensor_tensor(eb[:], Bv[:, :, 63], bst[:], ALU.subtract), [Bcr, bstr], [ebr])
                        c.op("act", lambda e: e.activation(eb[:], eb[:], AF.Exp), [ebr], [ebr])
                        for c8 in range(NCH // 8):
                            pt, ptr_ = ptr[c8 % 2]
                            for j in range(8):
                                cc = c8 * 8 + j
                                c.op("pe", lambda e: e.transpose(pt[:, j, :], khat[:, cc * 64:(cc + 1) * 64], identb[:]), [khatr, identbr], [ptr_])
                            copy_op(alt(), ktok[:, c8 * 8:(c8 + 1) * 8, :], pt[:], [ptr_], [ktokr])
                        c.op("dve", lambda e: e.memset(Sm[:], 0.0), [], [Smr])
                        for c4 in range(NCH // 4):
                            pd, pdr = pds[c4 % 2]
                            for j in range(4):
                                cc = c4 * 4 + j
                                c.op("pe", lambda e: e.matmul(pd[:, j, :], ktok[:, cc, :], itok[:, cc, :], start=True, stop=True), [ktokr, itokr], [pdr])
                            for j in range(4):
                                cc = c4 * 4 + j
                                c.op("dve", lambda e: e.scalar_tensor_tensor(Sm[:], Sm[:], eb[:, cc:cc + 1], pd[:, j, :], ALU.mult, ALU.add), [Smr, ebr, pdr], [Smr])
                                c.op("act", lambda e: e.activation(Sall[:, cc, :], Sm[:], AF.Copy), [Smr], [Sallr])
                        for c8 in range(NCH // 8):
                            pa, par = pat[0]
                            for j in range(8):
                                cc = c8 * 8 + j
                                c.op("pe", lambda e: e.matmul(pa[:, j, :], ktl[:, cc * 64:(cc + 1) * 64], qtl[:, cc * 64:(cc + 1) * 64], start=True, stop=True), [ktlr, qtlr], [par])
                            c.op("dve", lambda e: e.tensor_tensor(att[:, c8 * 8:(c8 + 1) * 8, :], pa[:], trimask[:].rearrange("p (c s) -> p c s", s=64), ALU.mult), [par, trimaskr], [attr])
                        for c8 in range(NCH // 8):
                            pp, ppr = po[c8 % 2]
                            for j in range(8):
                                cc = c8 * 8 + j
                                if cc > 0:
                                    c.op("pe", lambda e: e.matmul(pp[:, j, :], Sall[:, cc - 1, :], qin[:, cc * 64:(cc + 1) * 64], start=True, stop=False), [Sallr, qinr], [ppr])
                                c.op("pe", lambda e: e.matmul(pp[:, j, :], itok[:, cc, :], att[:, cc, :], start=(cc == 0), stop=True), [itokr, attr], [ppr])
                            ppf = pp[:].rearrange("p c s -> p (c s)")
                            (sq_, sqr), (rs_, rsr), (y_, yr), (yo_, yor) = sq[c8 % 2], rstd[c8 % 2], y1[c8 % 2], yo[c8 % 2]
                            c.op("act", lambda e: e.activation(sq_[:], ppf, AF.Square), [ppr], [sqr])
                            c.op("pe", lambda e: e.matmul(pss[0][0][:, :], onesb[:], sq_[:], start=True, stop=True), [onesbr, sqr], [pss[0][1]])
                            rms_rstd(pss[0][0][:, :], pss[0][1], rs_[:], rsr, 128)
                            c.op("dve", lambda e: e.scalar_tensor_tensor(y_[:], ppf, wn[:, hh:hh + 1], rs_[:], ALU.mult, ALU.mult), [ppr, wnr, rsr], [yr])
                            c.op("pool", lambda e: e.tensor_tensor(yo_[:], y_[:], sg[:, c8 * 512:(c8 + 1) * 512], ALU.mult), [yr, sgr], [yor])
                            c.dma("sp", S["hgoT"][fr, b * T + c8 * 512:b * T + (c8 + 1) * 512], yo_[:], [yor], [DR()])
                c.barrier()

        def stage_merge(l):
            with ExitStack() as st:
                wout, woutr = c.sb(st, "wout", [128, 8, D], BF16)
                wbr_ = [c.sb(st, "wbr%d" % bi, [128, 4, D], BF16) for bi in range(3)]
                with ExitStack() as tmp:
                    stg, stgr = c.sb(tmp, "mstg", [128, 8, D], F32)
                    for bi, nm in enumerate(("w_branch_sb", "w_branch_nsa", "w_branch_hgrn")):
                        wt, wtr = wbr_[bi]
                        c.dma("sp", stg[:, 0:4, :], W[nm][l].rearrange("(c p) n -> p c n", p=128), [], [stgr])
                        c.op(alt(("dve", "pool")), lambda e: e.tensor_copy(wt[:], stg[:, 0:4, :]), [stgr], [wtr])
                    c.dma("sp", stg[:], W["w_out"][l].rearrange("(c p) n -> p c n", p=128), [], [stgr])
                    c.op("dve", lambda e: e.tensor_copy(wout[:], stg[:]), [stgr], [woutr])
                    c.barrier()
                obT = [c.sb(st, "obT%d" % i, [128, 4, 512], BF16) for i in range(3)]
                gt = [c.sb(st, "gt%d" % i, [128, 512], BF16) for i in range(3)]
                mt = [c.sb(st, "mt%d" % i, [128, 512], F32) for i in range(2)]
                macc, maccr = c.sb(st, "macc", [128, 512], F32)
                mT, mTr = c.sb(st, "mT", [128, 8, 512], BF16)
                xt = [c.sb(st, "mxt%d" % i, [128, D], F32) for i in range(2)]
                pmm = [c.ps(st, "pmm%d" % i, [128, 512], F32) for i in range(3)]
                pout = [c.ps(st, "pout%d" % i, [128, 512], F32) for i in range(2)]
                srcs = (S["sboT"], S["nsaoT"], S["hgoT"])
                ig = 0
                ix = 0
                for tg in range(NG):
                    gs = slice(tg * 512, (tg + 1) * 512)
                    for bi in range(3):
                        c.dma("sp", obT[bi][0][:], srcs[bi][:, gs].rearrange("(c p) t -> p c t", p=128), [], [obT[bi][1]])
                    for fo in range(8):
                        for bi in range(3):
                            pp, ppr = pmm[bi]
                            g_, gr_ = gt[ig % 3]
                            ig += 1
                            c.dma("sp", g_[:], S["gmT"][bi * D + fo * 128:bi * D + (fo + 1) * 128, gs], [], [gr_])
                            for k in range(4):
                                c.op("pe", lambda e: e.matmul(pp[:, :], wbr_[bi][0][:, k, fo * 128:(fo + 1) * 128], obT[bi][0][:, k, :], start=(k == 0), stop=(k == 3)), [wbr_[bi][1], obT[bi][1]], [ppr])
                            if bi == 0:
                                c.op("dve", lambda e: e.tensor_tensor(macc[:], pp[:], g_[:], ALU.mult), [ppr, gr_], [maccr])
                            else:
                                m_, mr_ = mt[bi - 1]
                                c.op("dve", lambda e: e.tensor_tensor(m_[:], pp[:], g_[:], ALU.mult), [ppr, gr_], [mr_])
                                if bi == 1:
                                    c.op("pool", lambda e: e.tensor_tensor(macc[:], macc[:], m_[:], ALU.add), [maccr, mr_], [maccr])
                                else:
                                    c.op("pool", lambda e: e.tensor_tensor(mT[:, fo, :], macc[:], m_[:], ALU.add), [maccr, mr_], [mTr])
                    for j in range(4):
                        tt = tg * 4 + j
                        x_, xr = xt[ix % 2]
                        ix += 1
                        c.dma("sp", x_[:], xres[tt * 128:(tt + 1) * 128, :], [], [xr])
                        for cg in range(2):
                            pp, ppr = pout[cg]
                            for k in range(8):
                                c.op("pe", lambda e: e.matmul(pp[:, :], mT[:, k, j * 128:(j + 1) * 128], wout[:, k, cg * 512:(cg + 1) * 512], start=(k == 0), stop=(k == 7)), [mTr, woutr], [ppr])
                            c.op("dve", lambda e: e.tensor_tensor(x_[:, cg * 512:(cg + 1) * 512], x_[:, cg * 512:(cg + 1) * 512], pp[:, :], ALU.add), [xr, ppr], [xr])
                        c.dma("sp", xres[tt * 128:(tt + 1) * 128, :], x_[:], [xr], [DR()])
                c.barrier()

        def stage_peer(l):
            CR = 256
            for r0 in range(0, 16384, CR):
                c.dma("pool", S["uvbf"][r0:r0 + CR, 0:D], W["peer_u"][l][r0:r0 + CR, :], [], [DR()])
                c.dma("pool", S["uvbf"][r0:r0 + CR, D:2 * D], W["peer_v"][l][r0:r0 + CR, :], [], [DR()])
            c.barrier()
            with ExitStack() as st:
                wq, wqr = c.sb(st, "wq", [128, 8, D], BF16)
                skT, skTr = c.sb(st, "skT", [128, 128], BF16)
                g2, g2r = c.sb(st, "g2", [128, D], F32)
                with ExitStack() as tmp:
                    stg, stgr = c.sb(tmp, "pstg", [128, 8, D], F32)
                    sks, sksr = c.sb(tmp, "sks", [128, 128], F32)
                    c.dma("sp", stg[:], W["peer_w_q"][l].rearrange("(c p) n -> p c n", p=128), [], [stgr])
                    c.op("dve", lambda e: e.tensor_copy(wq[:], stg[:]), [stgr], [wqr])
                    for a in range(2):
                        c.dma("sp", sks[:, a * 64:(a + 1) * 64], W["peer_sub_keys"][l][a], [], [sksr])
                    pst, pstr = c.ps(tmp, "pst", [128, 128], F32)
                    c.op("pe", lambda e: e.transpose(pst[:], sks[:], identf[:]), [sksr, identfr], [pstr])
                    c.op("act", lambda e: e.activation(skT[:], pst[:], AF.Copy), [pstr], [skTr])
                    c.dma("sp", g2[:], W["norm_ffn"][l:l + 1, :].partition_broadcast(128), [], [g2r])
                    c.barrier()
                xt = [c.sb(st, "pxt%d" % i, [128, D], F32) for i in range(2)]
                h2 = [c.sb(st, "h2_%d" % i, [128, D], F32) for i in range(1)]
                h2b = [c.sb(st, "h2b%d" % i, [128, D], BF16) for i in range(2)]
                junkb, junkbr = c.sb(st, "pjunkb", [128, D], BF16)
                ss, ssr = c.sb(st, "pss", [128, 1], F32)
                rs, rsr = c.sb(st, "prs", [128, 1], F32)
                h2T, h2Tr = c.sb(st, "h2T", [128, 8, 128], BF16)
                qTs, qTsr = c.sb(st, "qTs", [128, 8, 128], BF16)
                sc, scr_ = c.sb(st, "sc", [128, 16, 128], F32)
                sc2, sc2r = c.sb(st, "sc2", [128, 128], F32)
                tops, topsr = c.sb(st, "tops", [128, 16, 16], F32)
                topi, topir = c.sb(st, "topi", [128, 16, 16], U32)
                topf, topfr = c.sb(st, "topf", [128, 16, 16], F32)
                t128, t128r = c.sb(st, "t128", [128, 8, 16], F32)
                cand, candr = c.sb(st, "cand", [128, 8, 256], F32)
                cand2, cand2r = c.sb(st, "cand2", [128, 256], F32)
                eid, eidr = c.sb(st, "eid", [128, 8, 256], F32)
                best, bestr = c.sb(st, "best", [128, 8, 16], F32)
                ej, ejr = c.sb(st, "ej", [128, 256], F32)
                eidxf, eidxfr = c.sb(st, "eidxf", [128, 128], F32)
                eidx = [c.sb(st, "eidx%d" % i, [128, 128], I32) for i in range(2)]
                gate = [c.sb(st, "gate%d" % i, [128, 8, 16], F32) for i in range(2)]
                gsum, gsumr = c.sb(st, "gsum", [128, 8], F32)
                hpre, hprer = c.sb(st, "hpre", [128, 128], F32)
                actv, actvr = c.sb(st, "actv", [128, 128], F32)
                NGB = 12
                gb = [c.sb(st, "gb%d" % i, [128, 2 * D], BF16) for i in range(NGB)]
                prod = [c.sb(st, "prod%d" % i, [128, D], BF16) for i in range(3)]
                dg = [c.sb(st, "dg%d" % i, [128, 128], BF16) for i in range(8)]
                ppT, ppTr = c.ps(st, "ppT", [128, 8, 128], BF16)
                pq = [c.ps(st, "pq%d" % i, [128, 4, 128], F32) for i in range(2)]
                psc = [c.ps(st, "psc%d" % i, [128, 4, 128], F32) for i in range(2)]
                py = [c.ps(st, "py%d" % i, [128, 512], F32) for i in range(2)]

                def prep(tt):
                    (x_, xr), (h_, hr), (hb_, hbr), (ei, eir), (gt_, gtr) = xt[tt % 2], h2[0], h2b[tt % 2], eidx[tt % 2], gate[tt % 2]
                    rows = slice(tt * 128, (tt + 1) * 128)
                    c.dma("sp", x_[:], xres[rows, :], [], [xr])
                    yield
                    c.op("act", lambda e: e.activation(junkb[:], x_[:], AF.Square, accum_out=ss[:]), [xr], [junkbr, ssr])
                    rms_rstd(ss[:], ssr, rs[:], rsr, D)
                    yield
                    c.op("dve", lambda e: e.scalar_tensor_tensor(h_[:], x_[:], rs[:], g2[:], ALU.mult, ALU.mult), [xr, rsr, g2r], [hr])
                    c.op("pool", lambda e: e.tensor_copy(hb_[:], h_[:]), [hr], [hbr])
                    yield
                    for k in range(8):
                        c.op("pe", lambda e: e.transpose(ppT[:, k, :], hb_[:, k * 128:(k + 1) * 128], identb[:]), [hbr, identbr], [ppTr])
                    c.op("act", lambda e: e.activation(h2T[:], ppT[:], AF.Copy), [ppTr], [h2Tr])
                    yield
                    for hg in range(2):
                        pp, ppr = pq[hg]
                        for hd in range(4):
                            hdd = hg * 4 + hd
                            for k in range(8):
                                c.op("pe", lambda e: e.matmul(pp[:, hd, :], wq[:, k, hdd * 128:(hdd + 1) * 128], h2T[:, k, :], start=(k == 0), stop=(k == 7)), [wqr, h2Tr], [ppr])
                            yield
                        copy_op(alt(), qTs[:, hg * 4:(hg + 1) * 4, :], pp[:], [ppr], [qTsr])
                        yield
                    scv = sc[:].rearrange("p (h a) k -> p h a k", a=2)
                    for s4 in range(4):
                        pp, ppr = psc[s4 % 2]
                        a, hg = s4 % 2, s4 // 2
                        for j in range(4):
                            hd = hg * 4 + j
                            c.op("pe", lambda e: e.matmul(pp[:, j, :], qTs[a * 64:(a + 1) * 64, hd, :], skT[a * 64:(a + 1) * 64, :], start=True, stop=True), [qTsr, skTr], [ppr])
                        copy_op(alt(), scv[:, hg * 4:(hg + 1) * 4, a, :], pp[:], [ppr], [scr_])
                        yield
                    for slot in range(16):
                        c.op("dve", lambda e: e.max(tops[:, slot, 0:8], sc[:, slot, :]), [scr_], [topsr])
                        c.op("dve", lambda e: e.max_index(topi[:, slot, 0:8], tops[:, slot, 0:8], sc[:, slot, :]), [scr_, topsr], [topir])
                        yield
                        c.op("dve", lambda e: e.match_replace(sc2[:], tops[:, slot, 0:8], sc[:, slot, :], NEG), [scr_, topsr], [sc2r])
                        c.op("dve", lambda e: e.max(tops[:, slot, 8:16], sc2[:]), [sc2r], [topsr])
                        yield
                        c.op("dve", lambda e: e.max_index(topi[:, slot, 8:16], tops[:, slot, 8:16], sc2[:]), [sc2r, topsr], [topir])
                        yield
                    c.op("dve", lambda e: e.tensor_copy(topf[:], topi[:]), [topir], [topfr])
                    tv = tops[:].rearrange("p (h a) k -> p h a k", a=2)
                    fv = topf[:].rearrange("p (h a) k -> p h a k", a=2)
                    cv = cand[:].rearrange("p h (i j) -> p h i j", j=16)
                    ev = eid[:].rearrange("p h (i j) -> p h i j", j=16)
                    c.op("dve", lambda e: e.tensor_tensor(cv, tv[:, :, 0, :].unsqueeze(3).to_broadcast([128, 8, 16, 16]), tv[:, :, 1, :].unsqueeze(2).to_broadcast([128, 8, 16, 16]), ALU.add), [topsr], [candr])
                    yield
                    c.op("dve", lambda e: e.tensor_scalar(t128[:], fv[:, :, 0, :], 128.0, None, ALU.mult), [topfr], [t128r])
                    c.op("dve", lambda e: e.tensor_tensor(ev, t128[:].unsqueeze(3).to_broadcast([128, 8, 16, 16]), fv[:, :, 1, :].unsqueeze(2).to_broadcast([128, 8, 16, 16]), ALU.add), [t128r, topfr], [eidr])
                    yield
                    for hd in range(8):
                        c.op("dve", lambda e: e.max(best[:, hd, 0:8], cand[:, hd, :]), [candr], [bestr])
                        c.op("dve", lambda e: e.match_replace(cand2[:], best[:, hd, 0:8], cand[:, hd, :], NEG), [candr, bestr], [cand2r])
                        c.op("dve", lambda e: e.max(best[:, hd, 8:16], cand2[:]), [cand2r], [bestr])
                        yield
                    for hd in range(8):
                        for kk in range(16):
                            j = hd * 16 + kk
                            c.op("dve", lambda e: e.scalar_tensor_tensor(ej[:], cand[:, hd, :], best[:, hd, kk:kk + 1], eid[:, hd, :], ALU.is_equal, ALU.mult, accum_out=eidxf[:, j:j + 1]), [candr, bestr, eidr], [eidxfr] if j in (0, 127) else [])
                            if kk % 2 == 1:
                                yield
                    c.op("dve", lambda e: e.tensor_scalar(eidxf[:], eidxf[:], 16383.0, 0.0, ALU.min, ALU.max), [eidxfr], [eidxfr])
                    c.op("dve", lambda e: e.tensor_copy(ei[:], eidxf[:]), [eidxfr], [eir])
                    yield
                    c.op("dve", lambda e: e.tensor_tensor(gt_[:], best[:], best[:, :, 0:1].to_broadcast([128, 8, 16]), ALU.subtract), [bestr], [gtr])
                    c.op("act", lambda e: e.activation(gt_[:], gt_[:], AF.Exp), [gtr], [gtr])
                    c.op("dve", lambda e: e.tensor_reduce(gsum[:], gt_[:], AX.X, ALU.add), [gtr], [gsumr])
                    yield
                    c.op("dve", lambda e: e.reciprocal(gsum[:], gsum[:]), [gsumr], [gsumr])
                    c.op("dve", lambda e: e.tensor_tensor(gt_[:], gt_[:], gsum[:].unsqueeze(2).to_broadcast([128, 8, 16]), ALU.mult), [gtr, gsumr], [gtr])
                    if "pdbg" in debug:
                        c.dma("sp", S["pdbg"][rows, 0, :], eidxf[:], [eidxfr], [DR()])
                        c.dma("sp", S["pdbg"][rows, 1, :], gt_[:].rearrange("p h k -> p (h k)"), [gtr], [DR()])
                        c.dma("sp", S["pdbg"][rows, 2, :], best[:].rearrange("p h k -> p (h k)"), [bestr], [DR()])
                    yield

                def drain(gen, n=None):
                    k = 0
                    while n is None or k < n:
                        try:
                            next(gen)
                        except StopIteration:
                            return
                        k += 1

                ig = [0]

                def gather_phase(tt, weave):
                    (x_, xr), (hb_, hbr), (ei, eir), (gt_, gtr) = xt[tt % 2], h2b[tt % 2], eidx[tt % 2], gate[tt % 2]
                    rows = slice(tt * 128, (tt + 1) * 128)
                    gtf = gt_[:].rearrange("p h k -> p (h k)")
                    for grp in range(16):
                        cols = slice(grp * 8, (grp + 1) * 8)
                        bufs = []
                        for jj in range(8):
                            j = grp * 8 + jj
                            g_, gr_ = gb[ig[0] % NGB]
                            ig[0] += 1
                            c.dma("pool", g_[:], S["uvbf"], [eir], [gr_], indirect=IndirectOffsetOnAxis(ei[:, j:j + 1], 0))
                            p_, pr_ = prod[j % 3]
                            c.op("dve", lambda e: e.tensor_tensor(p_[:], g_[:, 0:D], hb_[:], ALU.mult), [gr_, hbr], [pr_])
                            c.op("act", lambda e: e.activation(junkb[:], p_[:], AF.Copy, accum_out=hpre[:, j:j + 1]), [pr_], [hprer] if jj in (0, 7) else [])
                            bufs.append((g_, gr_))
                            drain(weave, 3)
                        c.op("act", lambda e: e.activation(actv[:, cols], hpre[:, cols], AF.Gelu), [hprer], [actvr])
                        c.op("dve", lambda e: e.tensor_tensor(actv[:, cols], actv[:, cols], gtf[:, cols], ALU.mult), [actvr, gtr], [actvr])
                        for jj in range(8):
                            j = grp * 8 + jj
                            g_, gr_ = bufs[jj]
                            d_, dr_ = dg[j % 8]
                            c.op("pool", lambda e: e.tensor_scalar(d_[:], identb[:], actv[:, j:j + 1], 1.0, ALU.mult, ALU.mult), [identbr, actvr], [dr_])
                            for hf in range(2):
                                c.op("pe", lambda e: e.matmul(py[hf][0][:, :], d_[:], g_[:, D + hf * 512:D + (hf + 1) * 512], start=(j == 0), stop=(j == 127)), [dr_, gr_], [py[hf][1]])
                    for hf in range(2):
                        c.op("dve", lambda e: e.tensor_tensor(x_[:, hf * 512:(hf + 1) * 512], x_[:, hf * 512:(hf + 1) * 512], py[hf][0][:, :], ALU.add), [xr, py[hf][1]], [xr])
                    c.dma("sp", xres[rows, :], x_[:], [xr], [DR()])

                drain(prep(0))
                for tt in range(NT):
                    weave = prep(tt + 1) if tt + 1 < NT else iter(())
                    gather_phase(tt, weave)
                    drain(weave)
                c.barrier()

        def stage_final():
            with ExitStack() as st:
                gf, gfr = c.sb(st, "gf", [128, D], F32)
                c.dma("sp", gf[:], W["norm_final"].rearrange("(o d) -> o d", o=1).partition_broadcast(128), [], [gfr])
                xt = [c.sb(st, "fxt%d" % i, [128, D], F32) for i in range(2)]
                junk, junkr = c.sb(st, "fjunk", [128, D], F32)
                ss = [c.sb(st, "fss%d" % i, [128, 1], F32) for i in range(2)]
                rs = [c.sb(st, "frs%d" % i, [128, 1], F32) for i in range(2)]
                for tt in range(NT):
                    (x_, xr), (s_, sr), (r_, rr_) = xt[tt % 2], ss[tt % 2], rs[tt % 2]
                    rows = slice(tt * 128, (tt + 1) * 128)
                    c.dma("sp", x_[:], xres[rows, :], [], [xr])
                    c.op("act", lambda e: e.activation(junk[:], x_[:], AF.Square, accum_out=s_[:]), [xr], [junkr, sr])
                    rms_rstd(s_[:], sr, r_[:], rr_, D)
                    c.op("dve", lambda e: e.scalar_tensor_tensor(x_[:], x_[:], r_[:], gf[:], ALU.mult, ALU.mult), [xr, rr_, gfr], [xr])
                    c.dma("sp", xres[rows, :], x_[:], [xr], [DR()])
                c.barrier()

        allst = stages if stages is not None else ("proj", "sb", "nsa", "hgrn", "merge", "peer", "final")
        for l in range(L):
            if "proj" in allst:
                stage_proj(l)
            if "sb" in allst:
                stage_sb(l)
            if "nsa" in allst:
                stage_nsa(l)
            if "hgrn" in allst:
                stage_hgrn(l)
            if "merge" in allst:
                stage_merge(l)
            if "xdbg" in debug and l == 0:
                c.dma("sp", S["xdbg"], xres, [], [DR()])
                c.barrier()
            if "peer" in allst:
                stage_peer(l)
        if "final" in allst:
            stage_final()
        c.barrier()
        build.n_ins = c.n_ins
    return nc


_CACHE = {}


def kernel(**inputs):
    cfg = Cfg(T=2048, NB=2, L=4)
    ncores = 8
    if "nc" not in _CACHE:
        _CACHE["nc"] = build(cfg)
    nc = _CACHE["nc"]
    consts = make_consts(cfg)
    x = np.ascontiguousarray(np.asarray(inputs["x"], dtype=np.float32))
    shared = {k: np.ascontiguousarray(np.asarray(inputs[k], dtype=np.float32)) for k in WEIGHT_SHAPES(cfg.L)}
    shared.update(consts)
    in_maps = []
    for i in range(ncores):
        m = dict(shared)
        m["x"] = x[i * cfg.NB:(i + 1) * cfg.NB].reshape(cfg.N, D)
        in_maps.append(m)
    res = run_bass_kernel_spmd(nc, in_maps, core_ids=list(range(ncores)))
    out = np.concatenate([np.asarray(r["out"]).reshape(cfg.NB, cfg.T, D) for r in res.results], axis=0)
    return out.astype(np.float32)
```

```python
from contextlib import ExitStack
import numpy as np
import concourse.bass as bass
import concourse.mybir as mybir
from concourse.bass import IndirectOffsetOnAxis
from concourse.bass_utils import run_bass_kernel_spmd

F32 = mybir.dt.float32
F32R = mybir.dt.float32r
BF16 = mybir.dt.bfloat16
U32 = mybir.dt.uint32
I32 = mybir.dt.int32
AF = mybir.ActivationFunctionType
ALU = mybir.AluOpType
AX = mybir.AxisListType

D = 1024
EPS = 1e-6
NEG = -1e30
FORCE = 1e4
IN_SPLITS = (512, 512, 512, 512, 768, 24, 512, 512, 512, 512, 3072)
IN_W = sum(IN_SPLITS)
NDS = 12


class Res:
    __slots__ = ("w", "r", "name")

    def __init__(self, name=""):
        self.w = None
        self.r = {}
        self.name = name


class Ctx:
    def __init__(self, nc, stack):
        self.nc = nc
        self.eng = {"pe": nc.tensor, "dve": nc.vector, "act": nc.scalar, "pool": nc.gpsimd, "sp": nc.sync}
        self.sem = {e: stack.enter_context(nc.semaphore("s_" + e)) for e in self.eng}
        self.cnt = {e: 0 for e in self.eng}
        self.seen = {e: {} for e in self.eng}
        self.dq = {q: [stack.enter_context(nc.semaphore("d_%s%d" % (q, i))) for i in range(NDS)] for q in ("sp", "act", "pool")}
        self.dqi = {q: 0 for q in self.dq}
        self.dq_uses = {q: [0] * NDS for q in self.dq}
        self.n_ins = 0
        self.uid = 0

    @staticmethod
    def _add(deps, st):
        key, sem, val = st
        cur = deps.get(key)
        if cur is None or cur[1] < val:
            deps[key] = (sem, val)

    def _deps(self, reads, writes):
        deps = {}
        for r in reads:
            if r.w is not None:
                self._add(deps, r.w)
        for w in writes:
            if w.w is not None:
                self._add(deps, w.w)
            for key, (sem, val) in w.r.items():
                self._add(deps, (key, sem, val))
        return deps

    def _wait(self, e, deps):
        seen = self.seen[e]
        for key, (sem, val) in deps.items():
            if key == "pe" and e == "pe":
                continue
            if seen.get(key, 0) >= val:
                continue
            self.eng[e].wait_ge(sem, val)
            seen[key] = val
            self.n_ins += 1

    def _stamp(self, st, reads, writes):
        key, sem, val = st
        for w in writes:
            w.w = st
            w.r = {}
        for r in reads:
            if r in writes:
                continue
            cur = r.r.get(key)
            if cur is None or cur[1] < val:
                r.r[key] = (sem, val)

    def op(self, e, fn, reads=(), writes=()):
        deps = self._deps(reads, writes)
        self._wait(e, deps)
        ins = fn(self.eng[e])
        self.cnt[e] += 1
        ins.then_inc(self.sem[e], 1)
        self.n_ins += 1
        self._stamp((e, self.sem[e], self.cnt[e]), reads, writes)
        return ins

    def dma(self, q, out, in_, reads=(), writes=(), indirect=None, **kw):
        deps = self._deps(reads, writes)
        i = self.dqi[q] % NDS
        self.dqi[q] += 1
        sem = self.dq[q][i]
        prev = self.dq_uses[q][i]
        key = ("d", q, i)
        if prev > 0:
            self._add(deps, (key, sem, 16 * prev))
        self._wait(q, deps)
        if indirect is not None:
            ins = self.eng[q].indirect_dma_start(out, None, in_, indirect, **kw)
        else:
            ins = self.eng[q].dma_start(out=out, in_=in_, **kw)
        ins.then_inc(sem, 16)
        self.n_ins += 1
        self.dq_uses[q][i] = prev + 1
        self._stamp((key, sem, 16 * (prev + 1)), reads, writes)
        return ins

    def barrier(self):
        deps = {}
        for e in self.eng:
            if self.cnt[e] > 0:
                deps[e] = (self.sem[e], self.cnt[e])
        for q in self.dq:
            for i in range(NDS):
                if self.dq_uses[q][i] > 0:
                    deps[("d", q, i)] = (self.dq[q][i], 16 * self.dq_uses[q][i])
        for e in self.eng:
            seen = self.seen[e]
            for key, (sem, val) in deps.items():
                if key == e or seen.get(key, 0) >= val:
                    continue
                self.eng[e].wait_ge(sem, val)
                seen[key] = val
                self.n_ins += 1

    def sb(self, stack, name, shape, dtype):
        self.uid += 1
        t = stack.enter_context(self.nc.sbuf_tensor("sb%d_%s" % (self.uid, name), list(shape), dtype))
        return t, Res(name)

    def ps(self, stack, name, shape, dtype=F32):
        self.uid += 1
        t = stack.enter_context(self.nc.psum_tensor("ps%d_%s" % (self.uid, name), list(shape), dtype))
        return t, Res(name)


class Cfg:
    def __init__(self, T=2048, NB=2, L=4):
        self.T = T
        self.NB = NB
        self.L = L
        self.N = NB * T
        self.NT = self.N // 128
        self.NCMP = T // 16 - 1
        self.NBLK = T // 64
        self.NCH = T // 64


def make_consts(cfg):
    T = cfg.T
    f = np.float32
    k = {}
    k["c_ident"] = np.eye(128, dtype=f)
    j = np.arange(128)
    k["c_ustr"] = (j[:, None] > j[None, :]).astype(f)
    ncmp, nblk = cfg.NCMP, cfg.NBLK
    cs = np.arange(ncmp) * 16
    bs = np.arange(nblk) * 64
    ov = np.zeros((128, 32), f)
    ov[:ncmp, :nblk] = ((cs[:, None] < bs[None, :] + 64) & (cs[:, None] + 32 > bs[None, :])).astype(f)
    k["c_overlap"] = ov
    s = np.arange(64)
    tm = (s[None, :] >= s[:, None]).astype(f)
    k["c_trimask"] = np.ascontiguousarray(np.broadcast_to(tm[:, None, :], (64, 8, 64))).reshape(64, 512)
    eg = np.zeros((24, 24, 128), f)
    for r in range(24):
        eg[r, r, :] = 1.0
    k["c_egate"] = eg.reshape(24, 24 * 128)
    BIGM = 1024.0
    es = np.zeros((33, 16, 128), f)
    for kb in range(16):
        es[2 * kb, kb, 0:64] = BIGM
        es[2 * kb + 1, kb, 64:128] = BIGM
    es[32, :, :] = -BIGM
    k["c_esel"] = es.reshape(33, 16 * 128)
    t = np.arange(T)
    slopes = 2.0 ** (-(np.arange(8) + 1.0))
    qa = np.zeros((4, 8, T), f)
    for h in range(8):
        sl = slopes[h] * 8.0
        qa[0, h] = 128.0 * sl
        qa[1, h] = sl
        qa[2, h] = -128.0 * sl * (t // 128)
        qa[3, h] = -sl * (t % 128)
    k["c_qaug"] = qa.reshape(4, 8 * T)
    ka = np.zeros((4, T), f)
    ka[0] = t // 128
    ka[1] = t % 128
    ka[2] = 1.0
    ka[3] = 1.0
    k["c_kaug"] = ka
    pc = np.arange(128) * 16 + 31
    kc = np.zeros((4, 128), f)
    kc[0] = pc // 128
    kc[1] = pc % 128
    kc[2] = 1.0
    kc[3] = 1.0
    k["c_kaugc"] = kc
    return k


CONST_SHAPES = lambda cfg: {
    "c_ident": [128, 128], "c_ustr": [128, 128], "c_overlap": [128, 32], "c_trimask": [64, 512],
    "c_egate": [24, 24 * 128], "c_esel": [33, 16 * 128], "c_qaug": [4, 8 * cfg.T], "c_kaug": [4, cfg.T],
    "c_kaugc": [4, 128],
}

WEIGHT_SHAPES = lambda L: {
    "norm_mix": [L, D], "norm_ffn": [L, D], "w_in": [L, D, IN_W], "nsa_w_cmp_k": [L, 2048, 64],
    "nsa_w_cmp_v": [L, 2048, 64], "nsa_cmp_pe": [L, 32, 64], "hgrn_norm": [L, 512],
    "hgrn_lower_bounds": [L, 512], "w_branch_sb": [L, 512, D], "w_branch_nsa": [L, 512, D],
    "w_branch_hgrn": [L, 512, D], "w_out": [L, D, D], "peer_w_q": [L, D, D],
    "peer_sub_keys": [L, 2, 128, 64], "peer_u": [L, 16384, D], "peer_v": [L, 16384, D], "norm_final": [D],
}


def build(cfg, stages=None, debug=()):
    nc = bass.Bass("TRN2", target_bir_lowering=False)
    T, NB, L, N, NT = cfg.T, cfg.NB, cfg.L, cfg.N, cfg.NT
    NG = N // 512
    NQT = T // 512
    NKB = T // 128
    NCMP, NBLK, NCH = cfg.NCMP, cfg.NBLK, cfg.NCH

    def din(name, shape, dt=F32):
        return nc.dram_tensor(name, list(shape), dt, kind="ExternalInput").ap()

    x_in = din("x", [N, D])
    W = {k: din(k, s) for k, s in WEIGHT_SHAPES(L).items()}
    CD = {k: din(k, s) for k, s in CONST_SHAPES(cfg).items()}
    xres = nc.dram_tensor("out", [N, D], F32, kind="ExternalOutput").ap()

    def scr(name, shape, dt):
        kind = "ExternalOutput" if name in debug else "Internal"
        return nc.dram_tensor(name, list(shape), dt, kind=kind).ap()

    S = {
        "sbqT": scr("sbqT", [512, N], BF16), "sbkT": scr("sbkT", [512, N], BF16), "sbv": scr("sbv", [N, 512], BF16),
        "nqT": scr("nqT", [512, N], BF16), "kcmpT": scr("kcmpT", [128, N], BF16), "vcmpT": scr("vcmpT", [128, N], BF16),
        "kselT": scr("kselT", [128, N], BF16), "vsel": scr("vsel", [N, 128], BF16), "kwinT": scr("kwinT", [128, N], BF16),
        "vwin": scr("vwin", [N, 128], BF16), "ngT": scr("ngT", [24, N], F32), "hqT": scr("hqT", [512, N], F32),
        "hfT": scr("hfT", [512, N], F32), "hi": scr("hi", [N, 512], BF16), "hgT": scr("hgT", [512, N], BF16),
        "gmT": scr("gmT", [3072, N], BF16),
        "sboT": scr("sboT", [512, N], BF16), "nsaoT": scr("nsaoT", [512, N], BF16), "hgoT": scr("hgoT", [512, N], BF16),
    }
    S["uvbf"] = scr("uvbf", [16384, 2 * D], BF16)
    if "xdbg" in debug:
        S["xdbg"] = scr("xdbg", [N, D], F32)
    if "pdbg" in debug:
        S["pdbg"] = scr("pdbg", [N, 4, 128], F32)

    DR = Res

    with ExitStack() as top:
        c = Ctx(nc, top)
        rr = [0]
        REG0 = nc.gpsimd.to_reg(0.0)
        REGF = nc.gpsimd.to_reg(FORCE)
        REGN = nc.gpsimd.to_reg(NEG)

        def alt(engs=("dve", "act")):
            rr[0] += 1
            return engs[rr[0] % len(engs)]

        def copy_op(eng, out, in_, reads, writes):
            if eng == "act":
                c.op("act", lambda e: e.activation(out, in_, AF.Copy), reads, writes)
            else:
                c.op(eng, lambda e: e.tensor_copy(out, in_), reads, writes)

        def load_const(name, shape, src):
            t, r = c.sb(top, name, shape, F32)
            c.dma("sp", t[:], src, [], [r])
            return t, r

        identf, identfr = load_const("identf", [128, 128], CD["c_ident"])
        ustrf, ustrfr = load_const("ustrf", [128, 128], CD["c_ustr"])
        overlap, overlapr = load_const("overlap", [128, 32], CD["c_overlap"])
        trimask, trimaskr = load_const("trimask", [64, 512], CD["c_trimask"])
        egate, egater = load_const("egate", [24, 24 * 128], CD["c_egate"])
        bfspecs = [("identb", [128, 128], "c_ident"), ("esel", [33, 16 * 128], "c_esel"), ("qaug", [4, 8 * T], "c_qaug"),
                   ("kaug", [4, T], "c_kaug"), ("kaugc", [4, 128], "c_kaugc")]
        bft = {}
        for name, shape, cn in bfspecs:
            bft[name] = c.sb(top, name, shape, BF16)
        identb, identbr = bft["identb"]
        esel, eselr = bft["esel"]
        qaug, qaugr = bft["qaug"]
        kaug, kaugr = bft["kaug"]
        kaugc, kaugcr = bft["kaugc"]
        onesf, onesfr = c.sb(top, "onesf", [128, 128], F32)
        onesb, onesbr = c.sb(top, "onesb", [128, 128], BF16)
        zerob, zerobr = c.sb(top, "zerob", [128, 512], BF16)
        c.op("pool", lambda e: e.memset(onesf[:], 1.0), [], [onesfr])
        c.op("pool", lambda e: e.memset(onesb[:], 1.0), [], [onesbr])
        c.op("pool", lambda e: e.memset(zerob[:], 0.0), [], [zerobr])
        lower, lowerr = c.sb(top, "lower", [128, 4, L], F32)
        oml, omlr = c.sb(top, "oml", [128, 4, L], F32)
        with ExitStack() as tmp:
            for name, shape, cn in bfspecs:
                stg, stgr = c.sb(tmp, "stg_" + name, shape, F32)
                c.dma("sp", stg[:], CD[cn], [], [stgr])
                c.op("pool", lambda e: e.tensor_copy(bft[name][0][:], stg[:]), [stgr], [bft[name][1]])
            c.barrier()
        with ExitStack() as tmp:
            lbr, lbrr = c.sb(tmp, "lbr", [128, 4, L], F32)
            lsum, lsumr = c.sb(tmp, "lsum", [128, 4], F32)
            with nc.allow_non_contiguous_dma(reason="tiny param layout"):
                for hh in range(4):
                    c.dma("sp", lbr[:, hh, :], W["hgrn_lower_bounds"][:, hh * 128:(hh + 1) * 128].rearrange("l k -> k l"), [], [lbrr])
            c.op("act", lambda e: e.activation(lbr[:], lbr[:], AF.Exp), [lbrr], [lbrr])
            c.op("dve", lambda e: e.tensor_reduce(lsum[:], lbr[:], AX.X, ALU.add), [lbrr], [lsumr])
            c.op("dve", lambda e: e.reciprocal(lsum[:], lsum[:]), [lsumr], [lsumr])
            c.op("dve", lambda e: e.tensor_tensor(lbr[:], lbr[:], lsum[:].unsqueeze(2).to_broadcast([128, 4, L]), ALU.mult), [lbrr, lsumr], [lbrr])
            c.op("dve", lambda e: e.memset(lower[:], 0.0), [], [lowerr])
            for l in range(1, L):
                c.op("dve", lambda e: e.tensor_tensor(lower[:, :, l], lower[:, :, l - 1], lbr[:, :, l], ALU.add), [lowerr, lbrr], [lowerr])
            c.op("dve", lambda e: e.tensor_scalar(oml[:], lower[:], -1.0, 1.0, ALU.mult, ALU.add), [lowerr], [omlr])
            c.barrier()
        c.dma("sp", xres, x_in, [], [DR()])
        c.barrier()

        def rms_rstd(ssum, ssr, rstd, rstdr, n):
            c.op("act", lambda e: e.activation(rstd, ssum, AF.Ln, bias=EPS, scale=1.0 / n), [ssr], [rstdr])
            c.op("act", lambda e: e.activation(rstd, rstd, AF.Exp, scale=-0.5), [rstdr], [rstdr])

        def stage_proj(l):
            with ExitStack() as st:
                hT, _ = c.sb(st, "hT", [128, 8, N], BF16)
                hTr = [Res() for _ in range(NG)]
                xt = [c.sb(st, "xt%d" % i, [128, D], F32) for i in range(2)]
                hb = [c.sb(st, "hb%d" % i, [128, D], BF16) for i in range(2)]
                junk, junkr = c.sb(st, "junk", [128, D], F32)
                ss = [c.sb(st, "ss%d" % i, [128, 1], F32) for i in range(2)]
                rs = [c.sb(st, "rs%d" % i, [128, 1], F32) for i in range(2)]
                pT = [c.ps(st, "pT%d" % i, [128, 8, 128], BF16) for i in range(2)]
                gcol, gcolr = c.sb(st, "gcol", [128, 8], F32)
                with nc.allow_non_contiguous_dma(reason="tiny param layout"):
                    c.dma("sp", gcol[:], W["norm_mix"][l].rearrange("(c p) -> p c", p=128), [], [gcolr])
                for tt in range(NT):
                    i = tt % 2
                    (x_, xr), (h_, hr), (s_, sr), (r_, rr_), (p_, pr) = xt[i], hb[i], ss[i], rs[i], pT[i]
                    c.dma("sp", x_[:], xres[tt * 128:(tt + 1) * 128, :], [], [xr])
                    c.op("act", lambda e: e.activation(junk[:], x_[:], AF.Square, accum_out=s_[:]), [xr], [junkr, sr])
                    rms_rstd(s_[:], sr, r_[:], rr_, D)
                    c.op("dve", lambda e: e.tensor_scalar(h_[:], x_[:], r_[:], None, ALU.mult), [xr, rr_], [hr])
                    for k in range(8):
                        c.op("pe", lambda e: e.transpose(p_[:, k, :], h_[:, k * 128:(k + 1) * 128], identb[:]), [hr, identbr], [pr])
                    copy_op(alt(), hT[:, :, tt * 128:(tt + 1) * 128], p_[:], [pr], [hTr[tt // 4]])

                wst = [c.sb(st, "wst%d" % i, [128, 8, 512], F32) for i in range(2)]
                wbf = [c.sb(st, "wbf%d" % i, [128, 8, 512], BF16) for i in range(2)]
                pb = [c.ps(st, "pb%d" % i, [128, 512], F32) for i in range(4)]
                ob16 = [c.sb(st, "ob16_%d" % i, [128, 512], BF16) for i in range(3)]
                ob32 = [c.sb(st, "ob32_%d" % i, [128, 512], F32) for i in range(2)]
                offs = np.cumsum((0,) + IN_SPLITS)
                jobs = [
                    (offs[0], 512, "fm", S["sbqT"], 0, BF16, None), (offs[1], 512, "fm", S["sbkT"], 0, BF16, None),
                    (offs[2], 512, "tm", S["sbv"], 0, BF16, None), (offs[3], 512, "fm", S["nqT"], 0, BF16, None),
                    (offs[4] + 0, 128, "fm", S["kcmpT"], 0, BF16, None), (offs[4] + 128, 128, "fm", S["vcmpT"], 0, BF16, None),
                    (offs[4] + 256, 128, "fm", S["kselT"], 0, BF16, None), (offs[4] + 384, 128, "tm", S["vsel"], 0, BF16, None),
                    (offs[4] + 512, 128, "fm", S["kwinT"], 0, BF16, None), (offs[4] + 640, 128, "tm", S["vwin"], 0, BF16, None),
                    (offs[5], 24, "fm", S["ngT"], 0, F32, None), (offs[6], 512, "fm", S["hqT"], 0, F32, AF.Silu),
                    (offs[7], 512, "fm", S["hfT"], 0, F32, AF.Sigmoid), (offs[8], 512, "tm", S["hi"], 0, BF16, None),
                    (offs[9], 512, "fm", S["hgT"], 0, BF16, AF.Silu),
                ] + [(offs[10] + 512 * j, 512, "fm", S["gmT"], 512 * j, BF16, AF.Sigmoid) for j in range(6)]
                cnt = [0, 0, 0]
                for ji, (col0, ncols, kind, dst, doff, dt, func) in enumerate(jobs):
                    col0 = int(col0)
                    (ws, wsr), (wb_, wbr) = wst[ji % 2], wbf[ji % 2]
                    c.dma("sp", ws[:, :, :ncols], W["w_in"][l][:, col0:col0 + ncols].rearrange("(c p) n -> p c n", p=128), [], [wsr])
                    c.op(alt(("dve", "pool")), lambda e: e.tensor_tensor(wb_[:, :, :ncols], ws[:, :, :ncols], gcol[:].unsqueeze(2).to_broadcast([128, 8, ncols]), ALU.mult), [wsr, gcolr], [wbr])

                    def evac(pp, ppr, rows, cols):
                        if dt == BF16:
                            o_, or_ = ob16[cnt[0] % 3]
                            cnt[0] += 1
                        else:
                            o_, or_ = ob32[cnt[1] % 2]
                            cnt[1] += 1
                        if func is not None:
                            c.op("act", lambda e: e.activation(o_[:rows, :cols], pp[:rows, :cols], func), [ppr], [or_])
                        else:
                            copy_op(alt(), o_[:rows, :cols], pp[:rows, :cols], [ppr], [or_])
                        return o_, or_

                    if kind == "tm":
                        for tt in range(NT):
                            pp, ppr = pb[cnt[2] % 4]
                            cnt[2] += 1
                            for k in range(8):
                                c.op("pe", lambda e: e.matmul(pp[:, :ncols], hT[:, k, tt * 128:(tt + 1) * 128], wb_[:, k, :ncols], start=(k == 0), stop=(k == 7)), [hTr[tt // 4], wbr], [ppr])
                            o_, or_ = evac(pp, ppr, 128, ncols)
                            c.dma("sp", dst[tt * 128:(tt + 1) * 128, doff:doff + ncols], o_[:, :ncols], [or_], [DR()])
                    else:
                        for f0 in range(0, ncols, 128):
                            F = min(128, ncols - f0)
                            for tg in range(NG):
                                pp, ppr = pb[cnt[2] % 4]
                                cnt[2] += 1
                                for k in range(8):
                                    c.op("pe", lambda e: e.matmul(pp[:F, :], wb_[:, k, f0:f0 + F], hT[:, k, tg * 512:(tg + 1) * 512], start=(k == 0), stop=(k == 7)), [hTr[tg], wbr], [ppr])
                                o_, or_ = evac(pp, ppr, F, 512)
                                c.dma("sp", dst[doff + f0:doff + f0 + F, tg * 512:(tg + 1) * 512], o_[:F, :], [or_], [DR()])
                c.barrier()

        def stage_sb(l):
            with ExitStack() as st:
                qT, qTr = c.sb(st, "qT", [128, 4, T], BF16)
                kT, kTr = c.sb(st, "kT", [128, 4, T], BF16)
                v, vr = c.sb(st, "v", [128, NKB, 512], BF16)
                ex = [c.sb(st, "ex%d" % i, [128, 512], F32) for i in range(3)]
                spl = [c.sb(st, "spl%d" % i, [128, 512], F32) for i in range(3)]
                t1 = [c.sb(st, "t1%d" % i, [128, 512], F32) for i in range(3)]
                at = [c.sb(st, "at%d" % i, [128, 512], BF16) for i in range(3)]
                M, Mr = c.sb(st, "M", [128, 512], F32)
                ustr_r, ustr_rr = c.sb(st, "ustr_r", [128, 128], F32)
                ones_r, ones_rr = c.sb(st, "ones_r", [128, 128], F32)
                c.op("dve", lambda e: e.tensor_copy(ustr_r[:].bitcast(F32R), ustrf[:]), [ustrfr], [ustr_rr])
                c.op("dve", lambda e: e.tensor_copy(ones_r[:].bitcast(F32R), onesf[:]), [onesfr], [ones_rr])
                osb = [c.sb(st, "osb%d" % i, [128, 512], BF16) for i in range(2)]
                pz = [c.ps(st, "pz%d" % i, [128, 512], F32) for i in range(3)]
                pcum = [c.ps(st, "pcum%d" % i, [128, 512], F32) for i in range(3)]
                pacc = [c.ps(st, "pacc%d" % i, [128, 512], F32) for i in range(2)]
                it = 0
                ia = 0
                for b in range(NB):
                    bs = slice(b * T, (b + 1) * T)
                    c.dma("sp", qT[:], S["sbqT"][:, bs].rearrange("(c p) t -> p c t", p=128), [], [qTr])
                    c.dma("sp", kT[:], S["sbkT"][:, bs].rearrange("(c p) t -> p c t", p=128), [], [kTr])
                    c.dma("sp", v[:], S["sbv"][bs, :].rearrange("(k p) f -> p k f", p=128), [], [vr])
                    for h in range(8):
                        ch, pb_ = h // 2, (h % 2) * 64
                        for qt in range(NQT):
                            t0 = qt * 512
                            nkb = (t0 + 512) // 128
                            acc, accr = pacc[ia % 2]
                            o_, or_ = osb[ia % 2]
                            ia += 1
                            c.op("pe", lambda e: e.matmul(acc[:, :], zerob[:, :128], zerob[:, :512], start=True, stop=False), [zerobr], [accr])
                            c.op("pool", lambda e: e.tensor_copy(M[:].bitcast(F32R), zerob[:]), [zerobr], [Mr])
                            for idx, kb in enumerate(reversed(range(nkb))):
                                s0 = kb * 128
                                tlo = max(0, s0 - t0)
                                diag = s0 >= t0
                                cs = slice(tlo, 512)
                                (z, zr), (cu, cur) = pz[it % 3], pcum[it % 3]
                                (e_, er), (sp_, spr), (t_, tr_), (a_, ar_) = ex[it % 3], spl[it % 3], t1[it % 3], at[it % 3]
                                it += 1
                                c.op("pe", lambda e: e.matmul(z[:, cs], kT[pb_:pb_ + 64, ch, s0:s0 + 128], qT[pb_:pb_ + 64, ch, t0 + tlo:t0 + 512], start=True, stop=True), [kTr, qTr], [zr])
                                c.op("act", lambda e: e.activation(e_[:, cs], z[:, cs], AF.Exp, scale=0.125), [zr], [er])
                                c.op("act", lambda e: e.activation(sp_[:, cs].bitcast(F32R), e_[:, cs], AF.Ln, bias=1.0), [er], [spr])
                                if diag:
                                    c.op("pool", lambda e: e.affine_select(sp_[:, tlo:tlo + 128].bitcast(F32R), sp_[:, tlo:tlo + 128], [[1, 128]], ALU.is_gt, REG0, base=0, channel_multiplier=-1), [spr], [spr])
                                c.op("pe", lambda e: e.matmul(cu[:, cs], ustr_r[:].bitcast(F32R), sp_[:, cs].bitcast(F32R), start=True, stop=(idx == 0)), [ustr_rr, spr], [cur])
                                if idx > 0:
                                    c.op("pe", lambda e: e.matmul(cu[:, cs], ones_r[:].bitcast(F32R), M[:, cs].bitcast(F32R), start=False, stop=True), [ones_rr, Mr], [cur])
                                c.op("dve", lambda e: e.scalar_tensor_tensor(t_[:, cs], z[:, cs], 0.125, sp_[:, cs], ALU.mult, ALU.subtract), [zr, spr], [tr_])
                                c.op("dve", lambda e: e.tensor_tensor(t_[:, cs], t_[:, cs], cu[:, cs], ALU.subtract), [tr_, cur], [tr_])
                                c.op("act", lambda e: e.activation(a_[:, cs], t_[:, cs], AF.Exp), [tr_], [ar_])
                                if diag:
                                    c.op("pool", lambda e: e.affine_select(a_[:, tlo:tlo + 128], a_[:, tlo:tlo + 128], [[1, 128]], ALU.is_gt, REG0, base=0, channel_multiplier=-1), [ar_], [ar_])
                                if kb > 0:
                                    c.op("pool", lambda e: e.tensor_tensor(M[:, cs].bitcast(F32R), M[:, cs], sp_[:, cs], ALU.add), [Mr, spr], [Mr])
                                c.op("pe", lambda e: e.matmul(acc[:, cs], v[:, kb, ch * 128:(ch + 1) * 128], a_[:, cs], start=False, stop=(kb == 0)), [vr, ar_], [accr])
                            copy_op(alt(), o_[pb_:pb_ + 64, :], acc[pb_:pb_ + 64, :], [accr], [or_])
                            c.dma("sp", S["sboT"][h * 64:(h + 1) * 64, b * T + t0:b * T + t0 + 512], o_[pb_:pb_ + 64, :], [or_], [DR()])
                    c.barrier()

        def stage_nsa(l):
            with ExitStack() as st:
                qT, qTr = c.sb(st, "nqT", [128, 4, T], BF16)
                ksel2 = [c.sb(st, "ksel2_%d" % g, [128, T], BF16) for g in range(2)]
                kwin2 = [c.sb(st, "kwin2_%d" % g, [128, T], BF16) for g in range(2)]
                vsel2, vsel2r = c.sb(st, "vsel2", [128, NKB, 2, 128], BF16)
                vwin2, vwin2r = c.sb(st, "vwin2", [128, NKB, 2, 128], BF16)
                gT, gTr = c.sb(st, "gT", [24, T], F32)
                nacc, _ = c.sb(st, "nacc", [128, 4, T], F32)
                naccr = [[Res() for _ in range(NQT)] for _ in range(8)]
                selT = [c.sb(st, "selT%d" % g, [33, T], BF16) for g in range(2)]
                selTr = [[Res() for _ in range(NQT)] for _ in range(2)]
                kc2 = [[c.sb(st, "kc2_%d_%d" % (b, g), [128, 128], BF16) for g in range(2)] for b in range(NB)]
                vc2 = [[c.sb(st, "vc2_%d_%d" % (b, g), [128, 128], F32) for g in range(2)] for b in range(NB)]
                with ExitStack() as tmp:
                    wk2, wk2r = c.sb(tmp, "wk2", [64, 32, 128], BF16)
                    wv2, wv2r = c.sb(tmp, "wv2", [64, 32, 128], BF16)
                    peT, peTr = c.sb(tmp, "peT", [64, 32], BF16)
                    wstg, wstgr = c.sb(tmp, "wstg", [64, 32, 64], F32)
                    pstg, pstgr = c.sb(tmp, "pstg", [64, 32], F32)
                    xcT = [c.sb(tmp, "xcT%d" % i, [64, T], BF16) for i in range(2)]
                    cb, cbr = c.sb(tmp, "cb", [128, 1], F32)
                    vcT, vcTr = c.sb(tmp, "vcT", [128, 128], F32)
                    pxc, pxcr = c.ps(tmp, "pxc", [128, 512], F32)
                    for (wsrc, wdst, wdr) in ((W["nsa_w_cmp_k"][l], wk2, wk2r), (W["nsa_w_cmp_v"][l], wv2, wv2r)):
                        c.dma("sp", wstg[:], wsrc.rearrange("(p d) e -> d p e", d=64), [], [wstgr])
                        c.op("dve", lambda e: e.tensor_copy(wdst[:, :, 0:64], wstg[:]), [wstgr], [wdr])
                        c.op("pool", lambda e: e.tensor_copy(wdst[:, :, 64:128], wstg[:]), [wstgr], [wdr])
                    with nc.allow_non_contiguous_dma(reason="tiny param layout"):
                        c.dma("sp", pstg[:], W["nsa_cmp_pe"][l].rearrange("p d -> d p"), [], [pstgr])
                    c.op("dve", lambda e: e.tensor_copy(peT[:], pstg[:]), [pstgr], [peTr])
                    for b in range(NB):
                        bs = slice(b * T, (b + 1) * T)
                        for g in range(2):
                            for wi, (src, w2, w2r) in enumerate(((S["kcmpT"], wk2, wk2r), (S["vcmpT"], wv2, wv2r))):
                                xc, xcr = xcT[wi]
                                c.dma("sp", xc[:], src[g * 64:(g + 1) * 64, bs], [], [xcr])
                                for p in range(32):
                                    c.op("pe", lambda e: e.matmul(pxc[:, 0:NCMP], w2[:, p, :], xc[:, p:p + 16 * (NCMP - 1) + 1:16], start=(p == 0), stop=(p == 31)), [w2r, xcr], [pxcr])
                                for p in range(32):
                                    c.op("pe", lambda e: e.matmul(pxc[:, 256:257], w2[:, p, :], peT[:, p:p + 1], start=(p == 0), stop=(p == 31)), [w2r, peTr], [pxcr])
                                c.op("dve", lambda e: e.tensor_copy(cb[:], pxc[:, 256:257]), [pxcr], [cbr])
                                if wi == 0:
                                    c.op("dve", lambda e: e.tensor_scalar(kc2[b][g][0][:, 0:NCMP], pxc[:, 0:NCMP], cb[:], None, ALU.add), [pxcr, cbr], [kc2[b][g][1]])
                                else:
                                    c.op("dve", lambda e: e.tensor_scalar(vcT[:, 0:NCMP], pxc[:, 0:NCMP], cb[:], None, ALU.add), [pxcr, cbr], [vcTr])
                                    c.op("pe", lambda e: e.transpose(pxc[0:NCMP, 0:128], vcT[:, 0:NCMP], identf[:]), [vcTr, identfr], [pxcr])
                                    c.op("act", lambda e: e.activation(vc2[b][g][0][0:NCMP, :], pxc[0:NCMP, 0:128], AF.Copy), [pxcr], [vc2[b][g][1]])
                    c.barrier()
                pcx = [c.sb(st, "pcx%d" % i, [128, 512], F32) for i in range(2)]
                pcn = [c.sb(st, "pcn%d" % i, [128, 512], F32) for i in range(1)] * 2
                rsb = [c.sb(st, "rsb%d" % i, [128, 512], F32) for i in range(2)]
                eg = [c.sb(st, "eg%d" % i, [128, 512], F32) for i in range(2)]
                wg = [c.sb(st, "wg%d" % i, [128, 512], F32) for i in range(2)]
                ctr = [c.sb(st, "ctr%d" % i, [128, 512], F32) for i in range(1)] * 2
                impT, impTr = c.sb(st, "impT", [32, 512], F32)
                imp, impr = c.sb(st, "imp", [128, 4, 32], F32)
                m8, m8r = c.sb(st, "m8", [128, 4, 8], F32)
                sel, selr = c.sb(st, "sel", [128, 4, 32], F32)
                pw = [c.sb(st, "pw%d" % i, [128, 512], BF16) for i in range(5)]
                pwm = [c.sb(st, "pwm%d" % i, [128, 512], BF16) for i in range(2)]
                osb = [c.sb(st, "nosb%d" % i, [128, 512], BF16) for i in range(2)]
                pz = [c.ps(st, "npz%d" % i, [128, 512], F32) for i in range(2)]
                pm, pmr = c.ps(st, "npm", [128, 512], F32)
                pacc, paccr = c.ps(st, "npacc", [128, 512], F32)
                prs, prsr = c.ps(st, "nprs", [128, 512], F32)
                pg, pgr = c.ps(st, "npg", [128, 512], F32)
                pimp, pimpr = c.ps(st, "npimp", [128, 512], F32)
                pxx, pxxr = c.ps(st, "npxx", [128, 512], F32)
                it = 0
                io = 0
                for g in range(2):
                    c.op("pool", lambda e: e.memset(selT[g][0][:], 0.0), [], [selTr[g][0]])
                    c.op("pool", lambda e: e.memset(selT[g][0][32:33, :], 1.0), [selTr[g][0]], [selTr[g][0]])
                c.barrier()
                zb5 = None
                for b in range(NB):
                    bs = slice(b * T, (b + 1) * T)
                    c.dma("sp", qT[:], S["nqT"][:, bs].rearrange("(c p) t -> p c t", p=128), [], [qTr])
                    for g in range(2):
                        for half in range(2):
                            c.dma("sp", ksel2[g][0][half * 64:(half + 1) * 64, :], S["kselT"][g * 64:(g + 1) * 64, bs], [], [ksel2[g][1]])
                            c.dma("sp", kwin2[g][0][half * 64:(half + 1) * 64, :], S["kwinT"][g * 64:(g + 1) * 64, bs], [], [kwin2[g][1]])
                            c.dma("sp", vsel2[:, :, g, half * 64:(half + 1) * 64], S["vsel"][bs, g * 64:(g + 1) * 64].rearrange("(k p) f -> p k f", p=128), [], [vsel2r])
                            c.dma("sp", vwin2[:, :, g, half * 64:(half + 1) * 64], S["vwin"][bs, g * 64:(g + 1) * 64].rearrange("(k p) f -> p k f", p=128), [], [vwin2r])
                    c.dma("sp", gT[:], S["ngT"][:, bs], [], [gTr])
                    for g in range(2):
                        for qt in range(NQT):
                            t0 = qt * 512
                            ts = slice(t0, t0 + 512)
                            for hp in range(4):
                                h = g * 4 + hp
                                ch, pb_ = h // 2, (h % 2) * 64
                                (z, zr) = pz[it % 2]
                                (px, pxr), (pn, pnr), (rb, rbr), (eg_, egr), (w_, wr_) = pcx[it % 2], pcn[it % 2], rsb[it % 2], eg[it % 2], wg[it % 2]
                                it += 1
                                c.op("pe", lambda e: e.matmul(z[0:NCMP, :], kc2[b][g][0][pb_:pb_ + 64, 0:NCMP], qT[pb_:pb_ + 64, ch, ts], start=True, stop=False), [kc2[b][g][1], qTr], [zr])
                                c.op("pe", lambda e: e.matmul(z[0:NCMP, :], kaugc[0:4, 0:NCMP], qaug[0:4, h * T + t0:h * T + t0 + 512], start=False, stop=True), [kaugcr, qaugr], [zr])
                                c.op("act", lambda e: e.activation(px[0:NCMP, :], z[0:NCMP, :], AF.Exp, scale=0.125), [zr], [pxr])
                                c.op("pool", lambda e: e.affine_select(px[0:NCMP, :], px[0:NCMP, :], [[1, 512]], ALU.is_ge, REG0, base=t0 - 31, channel_multiplier=-16), [pxr], [pxr])
                                c.op("pe", lambda e: e.matmul(prs[:, :], onesf[0:NCMP, :], px[0:NCMP, :], start=True, stop=True), [onesfr, pxr], [prsr])
                                c.op("pe", lambda e: e.matmul(pacc[:, :], vc2[b][g][0][0:NCMP, :], px[0:NCMP, :], start=True, stop=True), [vc2[b][g][1], pxr], [paccr])
                                c.op("pe", lambda e: e.matmul(pg[:, :], egate[0:24, (h * 3 + 0) * 128:(h * 3 + 1) * 128], gT[0:24, ts], start=True, stop=True), [egater, gTr], [pgr])
                                c.op("dve", lambda e: e.tensor_scalar(rb[:], prs[:], 1e-30, None, ALU.add), [prsr], [rbr])
                                c.op("dve", lambda e: e.reciprocal(rb[:], rb[:]), [rbr], [rbr])
                                c.op("dve", lambda e: e.tensor_tensor(pn[0:NCMP, :], px[0:NCMP, :], rb[0:NCMP, :], ALU.mult), [pxr, rbr], [pnr])
                                c.op("pe", lambda e: e.matmul(pimp[0:NBLK, :], overlap[0:NCMP, 0:NBLK], pn[0:NCMP, :], start=(hp == 0), stop=(hp == 3)), [overlapr, pnr], [pimpr])
                                c.op("act", lambda e: e.activation(eg_[:], pg[:], AF.Exp, scale=-1.0), [pgr], [egr])
                                c.op("dve", lambda e: e.tensor_scalar(eg_[:], eg_[:], 1.0, None, ALU.add), [egr], [egr])
                                c.op("dve", lambda e: e.reciprocal(eg_[:], eg_[:]), [egr], [egr])
                                c.op("dve", lambda e: e.tensor_tensor(w_[:], eg_[:], rb[:], ALU.mult), [egr, rbr], [wr_])
                                c.op("dve", lambda e: e.tensor_tensor(nacc[pb_:pb_ + 64, ch, ts], pacc[pb_:pb_ + 64, :], w_[pb_:pb_ + 64, :], ALU.mult), [paccr, wr_], [naccr[h][qt]])
                            c.op("act", lambda e: e.activation(impT[0:NBLK, :], pimp[0:NBLK, :], AF.Copy), [pimpr], [impTr])
                            for j in range(4):
                                c.op("pe", lambda e: e.transpose(pxx[:, j * 32:j * 32 + NBLK], impT[0:NBLK, j * 128:(j + 1) * 128], identf[0:NBLK, 0:NBLK]), [impTr, identfr], [pxxr])
                            c.op("act", lambda e: e.activation(imp[:, :, 0:NBLK], pxx[:, 0:128].rearrange("p (j n) -> p j n", n=32)[:, :, 0:NBLK], AF.Copy), [pxxr], [impr])
                            c.op("pool", lambda e: e.affine_select(imp[:, :, 0:NBLK], imp[:, :, 0:NBLK], [[128, 4], [-64, NBLK]], ALU.is_ge, REGF, base=t0 - 64, channel_multiplier=1), [impr], [impr])
                            c.op("pool", lambda e: e.affine_select(imp[:, :, 0:NBLK], imp[:, :, 0:NBLK], [[128, 4], [-64, NBLK]], ALU.is_ge, REGN, base=t0, channel_multiplier=1), [impr], [impr])
                            c.op("pool", lambda e: e.memset(imp[:, :, 0:1], FORCE), [impr], [impr])
                            for j in range(4):
                                c.op("dve", lambda e: e.max(m8[:, j, :], imp[:, j, 0:NBLK]), [impr], [m8r])
                            for j in range(4):
                                c.op("dve", lambda e: e.tensor_scalar(sel[:, j, 0:NBLK], imp[:, j, 0:NBLK], m8[:, j, 3:4], None, ALU.is_ge), [impr, m8r], [selr])
                            for j in range(4):
                                c.op("pe", lambda e: e.transpose(pxx[0:NBLK, j * 128:(j + 1) * 128], sel[:, j, 0:NBLK], identf[:]), [selr, identfr], [pxxr])
                            c.op("act", lambda e: e.activation(selT[g][0][0:NBLK, ts], pxx[0:NBLK, :], AF.Copy), [pxxr], [selTr[g][qt]])
                    for h in range(8):
                        g = h // 4
                        ch, pb_ = h // 2, (h % 2) * 64
                        for qt in range(NQT):
                            t0 = qt * 512
                            ts = slice(t0, t0 + 512)
                            for br in (2, 1):
                                c.op("pe", lambda e: e.matmul(pacc[:, :], zerob[:, :128], zerob[:, :512], start=True, stop=False), [zerobr], [paccr])
                                c.op("pe", lambda e: e.matmul(prs[:, :], zerob[:, :128], zerob[:, :512], start=True, stop=False), [zerobr], [prsr])
                                if br == 2:
                                    kbs = list(range(max(0, t0 // 128 - 2), t0 // 128 + 4))
                                    k2, k2r = kwin2[g]
                                    v2, v2r = vwin2, vwin2r
                                else:
                                    kbs = list(range(0, (t0 + 512) // 128))
                                    k2, k2r = ksel2[g]
                                    v2, v2r = vsel2, vsel2r
                                for ki, kb in enumerate(kbs):
                                    s0 = kb * 128
                                    tlo = max(0, s0 - t0)
                                    thi = min(512, s0 - t0 + 383) if br == 2 else 512
                                    cs = slice(tlo, thi)
                                    last = ki == len(kbs) - 1
                                    zb5 = [pz[0], pz[1], (pimp, pimpr), (pxx, pxxr), (pm, pmr)]
                                    (z, zr) = zb5[it % 5]
                                    (p_, pr_) = pw[it % 5]
                                    it += 1
                                    c.op("pe", lambda e: e.matmul(z[:, cs], k2[pb_:pb_ + 64, s0:s0 + 128], qT[pb_:pb_ + 64, ch, t0 + tlo:t0 + thi], start=True, stop=False), [k2r, qTr], [zr])
                                    c.op("pe", lambda e: e.matmul(z[:, cs], kaug[0:4, s0:s0 + 128], qaug[0:4, h * T + t0 + tlo:h * T + t0 + thi], start=False, stop=(br == 2)), [kaugr, qaugr], [zr])
                                    if br == 1:
                                        c.op("pe", lambda e: e.matmul(z[:, cs], esel[0:33, kb * 128:(kb + 1) * 128], selT[g][0][0:33, t0 + tlo:t0 + 512], start=False, stop=True), [eselr, selTr[g][qt]], [zr])
                                    c.op("act", lambda e: e.activation(p_[:, cs], z[:, cs], AF.Exp, scale=0.125), [zr], [pr_])
                                    if s0 >= t0:
                                        c.op("pool", lambda e: e.affine_select(p_[:, tlo:tlo + 128], p_[:, tlo:tlo + 128], [[1, 128]], ALU.is_ge, REG0, base=0, channel_multiplier=-1), [pr_], [pr_])
                                    if br == 2:
                                        r0 = s0 - t0 + 255
                                        a0, a1 = max(r0, 0), min(r0 + 128, 512)
                                        if a1 > a0:
                                            xs = a0 - r0
                                            c.op("pool", lambda e: e.affine_select(p_[:, a0:a1], p_[:, a0:a1], [[-1, a1 - a0]], ALU.is_ge, REG0, base=-xs, channel_multiplier=1), [pr_], [pr_])
                                    src, srcr = p_, pr_
                                    c.op("pe", lambda e: e.matmul(pacc[:, cs], v2[:, kb, g, :], src[:, cs], start=False, stop=last), [v2r, srcr], [paccr])
                                    c.op("pe", lambda e: e.matmul(prs[:, cs], onesb[:], src[:, cs], start=False, stop=last), [onesbr, srcr], [prsr])
                                (eg_, egr), (w_, wr_), (ct, ctr_) = eg[io % 2], wg[io % 2], ctr[io % 2]
                                io += 1
                                c.op("pe", lambda e: e.matmul(pg[:, :], egate[0:24, (h * 3 + br) * 128:(h * 3 + br + 1) * 128], gT[0:24, ts], start=True, stop=True), [egater, gTr], [pgr])
                                c.op("act", lambda e: e.activation(eg_[:], pg[:], AF.Exp, scale=-1.0), [pgr], [egr])
                                c.op("dve", lambda e: e.scalar_tensor_tensor(w_[:], eg_[:], 1.0, prs[:], ALU.add, ALU.mult), [egr, prsr], [wr_])
                                c.op("dve", lambda e: e.reciprocal(w_[:], w_[:]), [wr_], [wr_])
                                c.op("dve", lambda e: e.tensor_tensor(ct[pb_:pb_ + 64, :], pacc[pb_:pb_ + 64, :], w_[pb_:pb_ + 64, :], ALU.mult), [paccr, wr_], [ctr_])
                                c.op("pool", lambda e: e.tensor_tensor(nacc[pb_:pb_ + 64, ch, ts], nacc[pb_:pb_ + 64, ch, ts], ct[pb_:pb_ + 64, :], ALU.add), [naccr[h][qt], ctr_], [naccr[h][qt]])
                            o_, or_ = osb[io % 2]
                            c.op("act", lambda e: e.activation(o_[pb_:pb_ + 64, :], nacc[pb_:pb_ + 64, ch, ts], AF.Copy), [naccr[h][qt]], [or_])
                            c.dma("sp", S["nsaoT"][h * 64:(h + 1) * 64, b * T + t0:b * T + t0 + 512], o_[pb_:pb_ + 64, :], [or_], [DR()])
                    c.barrier()

        def stage_hgrn(l):
            with ExitStack() as st:
                ones_t, ones_tr = c.sb(st, "ones_t", [128, T], F32)
                c.op("pool", lambda e: e.memset(ones_t[:], 1.0), [], [ones_tr])
                wn, wnr = c.sb(st, "wn", [128, 4], F32)
                with nc.allow_non_contiguous_dma(reason="tiny param layout"):
                    c.dma("sp", wn[:], W["hgrn_norm"][l].rearrange("(h v) -> v h", v=128), [], [wnr])
                qh, qhr = c.sb(st, "qh", [128, T], F32)
                fs, fsr = c.sb(st, "fs", [128, T], F32)
                gl, glr = c.sb(st, "gl", [128, T], F32)
                Bc, Bcr = c.sb(st, "Bc", [128, T], F32)
                dd, ddr = c.sb(st, "dd", [128, T], F32)
                ee, eer = c.sb(st, "ee", [128, T], F32)
                qtl, qtlr = c.sb(st, "qtl", [128, T], BF16)
                ktl, ktlr = c.sb(st, "ktl", [128, T], BF16)
                qin, qinr = c.sb(st, "qin", [128, T], BF16)
                khat, khatr = c.sb(st, "khat", [128, T], BF16)
                bst, bstr = c.sb(st, "bst", [128, NCH], F32)
                eb, ebr = c.sb(st, "eb", [128, NCH], F32)
                itok, itokr = c.sb(st, "itok", [64, NCH, 128], BF16)
                sg, sgr = c.sb(st, "sg", [128, T], BF16)
                ktok, ktokr = c.sb(st, "ktok", [64, NCH, 128], BF16)
                Sm, Smr = c.sb(st, "Sm", [128, 128], F32)
                Sall, Sallr = c.sb(st, "Sall", [128, NCH, 128], BF16)
                att, attr = c.sb(st, "att", [64, NCH, 64], BF16)
                sq = [c.sb(st, "sq%d" % i, [128, 512], BF16) for i in range(2)]
                rstd = [c.sb(st, "hrstd%d" % i, [128, 512], F32) for i in range(2)]
                y1 = [c.sb(st, "y1_%d" % i, [128, 512], F32) for i in range(2)]
                yo = [c.sb(st, "yo%d" % i, [128, 512], BF16) for i in range(2)]
                ptr = [c.ps(st, "hptr%d" % i, [64, 8, 128], BF16) for i in range(2)]
                pds = [c.ps(st, "hpds%d" % i, [128, 4, 128], F32) for i in range(2)]
                pat = [c.ps(st, "hpat%d" % i, [64, 8, 64], F32) for i in range(1)]
                po = [c.ps(st, "hpo%d" % i, [128, 8, 64], F32) for i in range(2)]
                pss = [c.ps(st, "hpss%d" % i, [128, 512], F32) for i in range(1)]
                Bv = Bc[:].rearrange("p (c s) -> p c s", s=64)
                ddv = dd[:].rearrange("p (c s) -> p c s", s=64)
                for b in range(NB):
                    bs = slice(b * T, (b + 1) * T)
                    for hh in range(4):
                        fr = slice(hh * 128, (hh + 1) * 128)
                        c.dma("sp", qh[:], S["hqT"][fr, bs], [], [qhr])
                        c.dma("sp", fs[:], S["hfT"][fr, bs], [], [fsr])
                        c.dma("sp", itok[:], S["hi"][bs, fr].rearrange("(c s) v -> s c v", s=64), [], [itokr])
                        c.dma("sp", sg[:], S["hgT"][fr, bs], [], [sgr])
                        c.op("dve", lambda e: e.tensor_scalar(fs[:], fs[:], oml[:, hh, l:l + 1], lower[:, hh, l:l + 1], ALU.mult, ALU.add), [fsr, omlr, lowerr], [fsr])
                        c.op("act", lambda e: e.activation(gl[:], fs[:], AF.Ln), [fsr], [glr])
                        c.op("dve", lambda e: e.tensor_scalar(fs[:], fs[:], -1.0, 1.0, ALU.mult, ALU.add), [fsr, glr], [fsr])
                        c.op("dve", lambda e: e.tensor_tensor_scan(Bc[:], ones_t[:], gl[:], 0.0, ALU.mult, ALU.add), [ones_tr, glr], [Bcr])
                        c.op("dve", lambda e: e.tensor_tensor(ddv, Bv, Bv[:, :, 31:32].to_broadcast([128, NCH, 64]), ALU.subtract), [Bcr], [ddr])
                        c.op("act", lambda e: e.activation(ee[:], dd[:], AF.Exp), [ddr], [eer])
                        c.op("dve", lambda e: e.tensor_tensor(qtl[:], qh[:], ee[:], ALU.mult), [qhr, eer], [qtlr])
                        c.op("act", lambda e: e.activation(ee[:], dd[:], AF.Exp, scale=-1.0), [ddr, qtlr], [eer])
                        c.op("dve", lambda e: e.tensor_tensor(ktl[:], fs[:], ee[:], ALU.mult), [fsr, eer], [ktlr])
                        c.op("pool", lambda e: e.memset(bst[:, 0:1], 0.0), [], [bstr])
                        if NCH > 1:
                            c.op("pool", lambda e: e.tensor_copy(bst[:, 1:NCH], Bv[:, 0:NCH - 1, 63]), [Bcr], [bstr])
                        c.op("dve", lambda e: e.tensor_tensor(ddv, Bv, bst[:].unsqueeze(2).to_broadcast([128, NCH, 64]), ALU.subtract), [Bcr, bstr, ktlr], [ddr])
                        c.op("act", lambda e: e.activation(ee[:], dd[:], AF.Exp), [ddr, ktlr], [eer])
                        c.op("dve", lambda e: e.tensor_tensor(qin[:], qh[:], ee[:], ALU.mult), [qhr, eer], [qinr])
                        c.op("dve", lambda e: e.tensor_tensor(ddv, Bv, Bv[:, :, 63:64].to_broadcast([128, NCH, 64]), ALU.subtract), [Bcr, qinr], [ddr])
                        c.op("act", lambda e: e.activation(ee[:], dd[:], AF.Exp, scale=-1.0), [ddr, qinr], [eer])
                        c.op("dve", lambda e: e.tensor_tensor(khat[:], fs[:], ee[:], ALU.mult), [fsr, eer], [khatr])
                        c.op("dve", lambda e: e.tensor_tensor(eb[:], Bv[:, :, 63], bst[:], ALU.subtract), [Bcr, bstr], [ebr])
                        c.op("act", lambda e: e.activation(eb[:], eb[:], AF.Exp), [ebr], [ebr])
                        for c8 in range(NCH // 8):
                            pt, ptr_ = ptr[c8 % 2]
                            for j in range(8):
                                cc = c8 * 8 + j
                                c.op("pe", lambda e: e.transpose(pt[:, j, :], khat[:, cc * 64:(cc + 1) * 64], identb[:]), [khatr, identbr], [ptr_])
                            copy_op(alt(), ktok[:, c8 * 8:(c8 + 1) * 8, :], pt[:], [ptr_], [ktokr])
                        c.op("dve", lambda e: e.memset(Sm[:], 0.0), [], [Smr])
                        for c4 in range(NCH // 4):
                            pd, pdr = pds[c4 % 2]
                            for j in range(4):
                                cc = c4 * 4 + j
                                c.op("pe", lambda e: e.matmul(pd[:, j, :], ktok[:, cc, :], itok[:, cc, :], start=True, stop=True), [ktokr, itokr], [pdr])
                            for j in range(4):
                                cc = c4 * 4 + j
                                c.op("dve", lambda e: e.scalar_tensor_tensor(Sm[:], Sm[:], eb[:, cc:cc + 1], pd[:, j, :], ALU.mult, ALU.add), [Smr, ebr, pdr], [Smr])
                                c.op("act", lambda e: e.activation(Sall[:, cc, :], Sm[:], AF.Copy), [Smr], [Sallr])
                        for c8 in range(NCH // 8):
                            pa, par = pat[0]
                            for j in range(8):
                                cc = c8 * 8 + j
                                c.op("pe", lambda e: e.matmul(pa[:, j, :], ktl[:, cc * 64:(cc + 1) * 64], qtl[:, cc * 64:(cc + 1) * 64], start=True, stop=True), [ktlr, qtlr], [par])
                            c.op("dve", lambda e: e.tensor_tensor(att[:, c8 * 8:(c8 + 1) * 8, :], pa[:], trimask[:].rearrange("p (c s) -> p c s", s=64), ALU.mult), [par, trimaskr], [attr])
                        for c8 in range(NCH // 8):
                            pp, ppr = po[c8 % 2]
                            for j in range(8):
                                cc = c8 * 8 + j
                                if cc > 0:
                                    c.op("pe", lambda e: e.matmul(pp[:, j, :], Sall[:, cc - 1, :], qin[:, cc * 64:(cc + 1) * 64], start=True, stop=False), [Sallr, qinr], [ppr])
                                c.op("pe", lambda e: e.matmul(pp[:, j, :], itok[:, cc, :], att[:, cc, :], start=(cc == 0), stop=True), [itokr, attr], [ppr])
                            ppf = pp[:].rearrange("p c s -> p (c s)")
                            (sq_, sqr), (rs_, rsr), (y_, yr), (yo_, yor) = sq[c8 % 2], rstd[c8 % 2], y1[c8 % 2], yo[c8 % 2]
                            c.op("act", lambda e: e.activation(sq_[:], ppf, AF.Square), [ppr], [sqr])
                            c.op("pe", lambda e: e.matmul(pss[0][0][:, :], onesb[:], sq_[:], start=True, stop=True), [onesbr, sqr], [pss[0][1]])
                            rms_rstd(pss[0][0][:, :], pss[0][1], rs_[:], rsr, 128)
                            c.op("dve", lambda e: e.scalar_tensor_tensor(y_[:], ppf, wn[:, hh:hh + 1], rs_[:], ALU.mult, ALU.mult), [ppr, wnr, rsr], [yr])
                            c.op("pool", lambda e: e.tensor_tensor(yo_[:], y_[:], sg[:, c8 * 512:(c8 + 1) * 512], ALU.mult), [yr, sgr], [yor])
                            c.dma("sp", S["hgoT"][fr, b * T + c8 * 512:b * T + (c8 + 1) * 512], yo_[:], [yor], [DR()])
                c.barrier()

        def stage_merge(l):
            with ExitStack() as st:
                wout, woutr = c.sb(st, "wout", [128, 8, D], BF16)
                wbr_ = [c.sb(st, "wbr%d" % bi, [128, 4, D], BF16) for bi in range(3)]
                with ExitStack() as tmp:
                    stg, stgr = c.sb(tmp, "mstg", [128, 8, D], F32)
                    for bi, nm in enumerate(("w_branch_sb", "w_branch_nsa", "w_branch_hgrn")):
                        wt, wtr = wbr_[bi]
                        c.dma("sp", stg[:, 0:4, :], W[nm][l].rearrange("(c p) n -> p c n", p=128), [], [stgr])
                        c.op(alt(("dve", "pool")), lambda e: e.tensor_copy(wt[:], stg[:, 0:4, :]), [stgr], [wtr])
                    c.dma("sp", stg[:], W["w_out"][l].rearrange("(c p) n -> p c n", p=128), [], [stgr])
                    c.op("dve", lambda e: e.tensor_copy(wout[:], stg[:]), [stgr], [woutr])
                    c.barrier()
                obT = [c.sb(st, "obT%d" % i, [128, 4, 512], BF16) for i in range(3)]
                gt = [c.sb(st, "gt%d" % i, [128, 512], BF16) for i in range(3)]
                mt = [c.sb(st, "mt%d" % i, [128, 512], F32) for i in range(2)]
                macc, maccr = c.sb(st, "macc", [128, 512], F32)
                mT, mTr = c.sb(st, "mT", [128, 8, 512], BF16)
                xt = [c.sb(st, "mxt%d" % i, [128, D], F32) for i in range(2)]
                pmm = [c.ps(st, "pmm%d" % i, [128, 512], F32) for i in range(3)]
                pout = [c.ps(st, "pout%d" % i, [128, 512], F32) for i in range(2)]
                srcs = (S["sboT"], S["nsaoT"], S["hgoT"])
                ig = 0
                ix = 0
                for tg in range(NG):
                    gs = slice(tg * 512, (tg + 1) * 512)
                    for bi in range(3):
                        c.dma("sp", obT[bi][0][:], srcs[bi][:, gs].rearrange("(c p) t -> p c t", p=128), [], [obT[bi][1]])
                    for fo in range(8):
                        for bi in range(3):
                            pp, ppr = pmm[bi]
                            g_, gr_ = gt[ig % 3]
                            ig += 1
                            c.dma("sp", g_[:], S["gmT"][bi * D + fo * 128:bi * D + (fo + 1) * 128, gs], [], [gr_])
                            for k in range(4):
                                c.op("pe", lambda e: e.matmul(pp[:, :], wbr_[bi][0][:, k, fo * 128:(fo + 1) * 128], obT[bi][0][:, k, :], start=(k == 0), stop=(k == 3)), [wbr_[bi][1], obT[bi][1]], [ppr])
                            if bi == 0:
                                c.op("dve", lambda e: e.tensor_tensor(macc[:], pp[:], g_[:], ALU.mult), [ppr, gr_], [maccr])
                            else:
                                m_, mr_ = mt[bi - 1]
                                c.op("dve", lambda e: e.tensor_tensor(m_[:], pp[:], g_[:], ALU.mult), [ppr, gr_], [mr_])
                                if bi == 1:
                                    c.op("pool", lambda e: e.tensor_tensor(macc[:], macc[:], m_[:], ALU.add), [maccr, mr_], [maccr])
                                else:
                                    c.op("pool", lambda e: e.tensor_tensor(mT[:, fo, :], macc[:], m_[:], ALU.add), [maccr, mr_], [mTr])
                    for j in range(4):
                        tt = tg * 4 + j
                        x_, xr = xt[ix % 2]
                        ix += 1
                        c.dma("sp", x_[:], xres[tt * 128:(tt + 1) * 128, :], [], [xr])
                        for cg in range(2):
                            pp, ppr = pout[cg]
                            for k in range(8):
                                c.op("pe", lambda e: e.matmul(pp[:, :], mT[:, k, j * 128:(j + 1) * 128], wout[:, k, cg * 512:(cg + 1) * 512], start=(k == 0), stop=(k == 7)), [mTr, woutr], [ppr])
                            c.op("dve", lambda e: e.tensor_tensor(x_[:, cg * 512:(cg + 1) * 512], x_[:, cg * 512:(cg + 1) * 512], pp[:, :], ALU.add), [xr, ppr], [xr])
                        c.dma("sp", xres[tt * 128:(tt + 1) * 128, :], x_[:], [xr], [DR()])
                c.barrier()

        def stage_peer(l):
            CR = 256
            for r0 in range(0, 16384, CR):
                c.dma("pool", S["uvbf"][r0:r0 + CR, 0:D], W["peer_u"][l][r0:r0 + CR, :], [], [DR()])
                c.dma("pool", S["uvbf"][r0:r0 + CR, D:2 * D], W["peer_v"][l][r0:r0 + CR, :], [], [DR()])
            c.barrier()
            with ExitStack() as st:
                wq, wqr = c.sb(st, "wq", [128, 8, D], BF16)
                skT, skTr = c.sb(st, "skT", [128, 128], BF16)
                g2, g2r = c.sb(st, "g2", [128, D], F32)
                with ExitStack() as tmp:
                    stg, stgr = c.sb(tmp, "pstg", [128, 8, D], F32)
                    sks, sksr = c.sb(tmp, "sks", [128, 128], F32)
                    c.dma("sp", stg[:], W["peer_w_q"][l].rearrange("(c p) n -> p c n", p=128), [], [stgr])
                    c.op("dve", lambda e: e.tensor_copy(wq[:], stg[:]), [stgr], [wqr])
                    for a in range(2):
                        c.dma("sp", sks[:, a * 64:(a + 1) * 64], W["peer_sub_keys"][l][a], [], [sksr])
                    pst, pstr = c.ps(tmp, "pst", [128, 128], F32)
                    c.op("pe", lambda e: e.transpose(pst[:], sks[:], identf[:]), [sksr, identfr], [pstr])
                    c.op("act", lambda e: e.activation(skT[:], pst[:], AF.Copy), [pstr], [skTr])
                    c.dma("sp", g2[:], W["norm_ffn"][l:l + 1, :].partition_broadcast(128), [], [g2r])
                    c.barrier()
                xt = [c.sb(st, "pxt%d" % i, [128, D], F32) for i in range(2)]
                h2 = [c.sb(st, "h2_%d" % i, [128, D], F32) for i in range(1)]
                h2b = [c.sb(st, "h2b%d" % i, [128, D], BF16) for i in range(2)]
                junkb, junkbr = c.sb(st, "pjunkb", [128, D], BF16)
                ss, ssr = c.sb(st, "pss", [128, 1], F32)
                rs, rsr = c.sb(st, "prs", [128, 1], F32)
                h2T, h2Tr = c.sb(st, "h2T", [128, 8, 128], BF16)
                qTs, qTsr = c.sb(st, "qTs", [128, 8, 128], BF16)
                sc, scr_ = c.sb(st, "sc", [128, 16, 128], F32)
                sc2, sc2r = c.sb(st, "sc2", [128, 128], F32)
                tops, topsr = c.sb(st, "tops", [128, 16, 16], F32)
                topi, topir = c.sb(st, "topi", [128, 16, 16], U32)
                topf, topfr = c.sb(st, "topf", [128, 16, 16], F32)
                t128, t128r = c.sb(st, "t128", [128, 8, 16], F32)
                cand, candr = c.sb(st, "cand", [128, 8, 256], F32)
                cand2, cand2r = c.sb(st, "cand2", [128, 256], F32)
                eid, eidr = c.sb(st, "eid", [128, 8, 256], F32)
                best, bestr = c.sb(st, "best", [128, 8, 16], F32)
                ej, ejr = c.sb(st, "ej", [128, 256], F32)
                eidxf, eidxfr = c.sb(st, "eidxf", [128, 128], F32)
                eidx = [c.sb(st, "eidx%d" % i, [128, 128], I32) for i in range(2)]
                gate = [c.sb(st, "gate%d" % i, [128, 8, 16], F32) for i in range(2)]
                gsum, gsumr = c.sb(st, "gsum", [128, 8], F32)
                hpre, hprer = c.sb(st, "hpre", [128, 128], F32)
                actv, actvr = c.sb(st, "actv", [128, 128], F32)
                NGB = 12
                gb = [c.sb(st, "gb%d" % i, [128, 2 * D], BF16) for i in range(NGB)]
                prod = [c.sb(st, "prod%d" % i, [128, D], BF16) for i in range(3)]
                dg = [c.sb(st, "dg%d" % i, [128, 128], BF16) for i in range(8)]
                ppT, ppTr = c.ps(st, "ppT", [128, 8, 128], BF16)
                pq = [c.ps(st, "pq%d" % i, [128, 4, 128], F32) for i in range(2)]
                psc = [c.ps(st, "psc%d" % i, [128, 4, 128], F32) for i in range(2)]
                py = [c.ps(st, "py%d" % i, [128, 512], F32) for i in range(2)]

                def prep(tt):
                    (x_, xr), (h_, hr), (hb_, hbr), (ei, eir), (gt_, gtr) = xt[tt % 2], h2[0], h2b[tt % 2], eidx[tt % 2], gate[tt % 2]
                    rows = slice(tt * 128, (tt + 1) * 128)
                    c.dma("sp", x_[:], xres[rows, :], [], [xr])
                    yield
                    c.op("act", lambda e: e.activation(junkb[:], x_[:], AF.Square, accum_out=ss[:]), [xr], [junkbr, ssr])
                    rms_rstd(ss[:], ssr, rs[:], rsr, D)
                    yield
                    c.op("dve", lambda e: e.scalar_tensor_tensor(h_[:], x_[:], rs[:], g2[:], ALU.mult, ALU.mult), [xr, rsr, g2r], [hr])
                    c.op("pool", lambda e: e.tensor_copy(hb_[:], h_[:]), [hr], [hbr])
                    yield
                    for k in range(8):
                        c.op("pe", lambda e: e.transpose(ppT[:, k, :], hb_[:, k * 128:(k + 1) * 128], identb[:]), [hbr, identbr], [ppTr])
                    c.op("act", lambda e: e.activation(h2T[:], ppT[:], AF.Copy), [ppTr], [h2Tr])
                    yield
                    for hg in range(2):
                        pp, ppr = pq[hg]
                        for hd in range(4):
                            hdd = hg * 4 + hd
                            for k in range(8):
                                c.op("pe", lambda e: e.matmul(pp[:, hd, :], wq[:, k, hdd * 128:(hdd + 1) * 128], h2T[:, k, :], start=(k == 0), stop=(k == 7)), [wqr, h2Tr], [ppr])
                            yield
                        copy_op(alt(), qTs[:, hg * 4:(hg + 1) * 4, :], pp[:], [ppr], [qTsr])
                        yield
                    scv = sc[:].rearrange("p (h a) k -> p h a k", a=2)
                    for s4 in range(4):
                        pp, ppr = psc[s4 % 2]
                        a, hg = s4 % 2, s4 // 2
                        for j in range(4):
                            hd = hg * 4 + j
                            c.op("pe", lambda e: e.matmul(pp[:, j, :], qTs[a * 64:(a + 1) * 64, hd, :], skT[a * 64:(a + 1) * 64, :], start=True, stop=True), [qTsr, skTr], [ppr])
                        copy_op(alt(), scv[:, hg * 4:(hg + 1) * 4, a, :], pp[:], [ppr], [scr_])
                        yield
                    for slot in range(16):
                        c.op("dve", lambda e: e.max(tops[:, slot, 0:8], sc[:, slot, :]), [scr_], [topsr])
                        c.op("dve", lambda e: e.max_index(topi[:, slot, 0:8], tops[:, slot, 0:8], sc[:, slot, :]), [scr_, topsr], [topir])
                        yield
                        c.op("dve", lambda e: e.match_replace(sc2[:], tops[:, slot, 0:8], sc[:, slot, :], NEG), [scr_, topsr], [sc2r])
                        c.op("dve", lambda e: e.max(tops[:, slot, 8:16], sc2[:]), [sc2r], [topsr])
                        yield
                        c.op("dve", lambda e: e.max_index(topi[:, slot, 8:16], tops[:, slot, 8:16], sc2[:]), [sc2r, topsr], [topir])
                        yield
                    c.op("dve", lambda e: e.tensor_copy(topf[:], topi[:]), [topir], [topfr])
                    tv = tops[:].rearrange("p (h a) k -> p h a k", a=2)
                    fv = topf[:].rearrange("p (h a) k -> p h a k", a=2)
                    cv = cand[:].rearrange("p h (i j) -> p h i j", j=16)
                    ev = eid[:].rearrange("p h (i j) -> p h i j", j=16)
                    c.op("dve", lambda e: e.tensor_tensor(cv, tv[:, :, 0, :].unsqueeze(3).to_broadcast([128, 8, 16, 16]), tv[:, :, 1, :].unsqueeze(2).to_broadcast([128, 8, 16, 16]), ALU.add), [topsr], [candr])
                    yield
                    c.op("dve", lambda e: e.tensor_scalar(t128[:], fv[:, :, 0, :], 128.0, None, ALU.mult), [topfr], [t128r])
                    c.op("dve", lambda e: e.tensor_tensor(ev, t128[:].unsqueeze(3).to_broadcast([128, 8, 16, 16]), fv[:, :, 1, :].unsqueeze(2).to_broadcast([128, 8, 16, 16]), ALU.add), [t128r, topfr], [eidr])
                    yield
                    for hd in range(8):
                        c.op("dve", lambda e: e.max(best[:, hd, 0:8], cand[:, hd, :]), [candr], [bestr])
                        c.op("dve", lambda e: e.match_replace(cand2[:], best[:, hd, 0:8], cand[:, hd, :], NEG), [candr, bestr], [cand2r])
                        c.op("dve", lambda e: e.max(best[:, hd, 8:16], cand2[:]), [cand2r], [bestr])
                        yield
                    for hd in range(8):
                        for kk in range(16):
                            j = hd * 16 + kk
                            c.op("dve", lambda e: e.scalar_tensor_tensor(ej[:], cand[:, hd, :], best[:, hd, kk:kk + 1], eid[:, hd, :], ALU.is_equal, ALU.mult, accum_out=eidxf[:, j:j + 1]), [candr, bestr, eidr], [eidxfr] if j in (0, 127) else [])
                            if kk % 2 == 1:
                                yield
                    c.op("dve", lambda e: e.tensor_scalar(eidxf[:], eidxf[:], 16383.0, 0.0, ALU.min, ALU.max), [eidxfr], [eidxfr])
                    c.op("dve", lambda e: e.tensor_copy(ei[:], eidxf[:]), [eidxfr], [eir])
                    yield
                    c.op("dve", lambda e: e.tensor_tensor(gt_[:], best[:], best[:, :, 0:1].to_broadcast([128, 8, 16]), ALU.subtract), [bestr], [gtr])
                    c.op("act", lambda e: e.activation(gt_[:], gt_[:], AF.Exp), [gtr], [gtr])
                    c.op("dve", lambda e: e.tensor_reduce(gsum[:], gt_[:], AX.X, ALU.add), [gtr], [gsumr])
                    yield
                    c.op("dve", lambda e: e.reciprocal(gsum[:], gsum[:]), [gsumr], [gsumr])
                    c.op("dve", lambda e: e.tensor_tensor(gt_[:], gt_[:], gsum[:].unsqueeze(2).to_broadcast([128, 8, 16]), ALU.mult), [gtr, gsumr], [gtr])
                    if "pdbg" in debug:
                        c.dma("sp", S["pdbg"][rows, 0, :], eidxf[:], [eidxfr], [DR()])
                        c.dma("sp", S["pdbg"][rows, 1, :], gt_[:].rearrange("p h k -> p (h k)"), [gtr], [DR()])
                        c.dma("sp", S["pdbg"][rows, 2, :], best[:].rearrange("p h k -> p (h k)"), [bestr], [DR()])
                    yield

                def drain(gen, n=None):
                    k = 0
                    while n is None or k < n:
                        try:
                            next(gen)
                        except StopIteration:
                            return
                        k += 1

                ig = [0]
                GS = 4

                def gather_phase(tt, weave):
                    (x_, xr), (hb_, hbr), (ei, eir), (gt_, gtr) = xt[tt % 2], h2b[tt % 2], eidx[tt % 2], gate[tt % 2]
                    rows = slice(tt * 128, (tt + 1) * 128)
                    gtf = gt_[:].rearrange("p h k -> p (h k)")
                    for grp in range(128 // GS):
                        cols = slice(grp * GS, (grp + 1) * GS)
                        bufs = []
                        for jj in range(GS):
                            j = grp * GS + jj
                            g_, gr_ = gb[ig[0] % NGB]
                            ig[0] += 1
                            c.dma("pool", g_[:], S["uvbf"], [eir], [gr_], indirect=IndirectOffsetOnAxis(ei[:, j:j + 1], 0))
                            p_, pr_ = prod[j % 3]
                            c.op("dve", lambda e: e.tensor_tensor(p_[:], g_[:, 0:D], hb_[:], ALU.mult), [gr_, hbr], [pr_])
                            c.op("act", lambda e: e.activation(junkb[:], p_[:], AF.Copy, accum_out=hpre[:, j:j + 1]), [pr_], [hprer] if jj in (0, GS - 1) else [])
                            bufs.append((g_, gr_))
                            drain(weave, 3)
                        c.op("act", lambda e: e.activation(actv[:, cols], hpre[:, cols], AF.Gelu), [hprer], [actvr])
                        c.op("dve", lambda e: e.tensor_tensor(actv[:, cols], actv[:, cols], gtf[:, cols], ALU.mult), [actvr, gtr], [actvr])
                        for jj in range(GS):
                            j = grp * GS + jj
                            g_, gr_ = bufs[jj]
                            d_, dr_ = dg[j % 8]
                            c.op("dve", lambda e: e.tensor_scalar(d_[:], identb[:], actv[:, j:j + 1], None, ALU.mult), [identbr, actvr], [dr_])
                            for hf in range(2):
                                c.op("pe", lambda e: e.matmul(py[hf][0][:, :], d_[:], g_[:, D + hf * 512:D + (hf + 1) * 512], start=(j == 0), stop=(j == 127)), [dr_, gr_], [py[hf][1]])
                    for hf in range(2):
                        c.op("dve", lambda e: e.tensor_tensor(x_[:, hf * 512:(hf + 1) * 512], x_[:, hf * 512:(hf + 1) * 512], py[hf][0][:, :], ALU.add), [xr, py[hf][1]], [xr])
                    c.dma("sp", xres[rows, :], x_[:], [xr], [DR()])

                drain(prep(0))
                for tt in range(NT):
                    weave = prep(tt + 1) if tt + 1 < NT else iter(())
                    gather_phase(tt, weave)
                    drain(weave)
                c.barrier()

        def stage_final():
            with ExitStack() as st:
                gf, gfr = c.sb(st, "gf", [128, D], F32)
                c.dma("sp", gf[:], W["norm_final"].rearrange("(o d) -> o d", o=1).partition_broadcast(128), [], [gfr])
                xt = [c.sb(st, "fxt%d" % i, [128, D], F32) for i in range(2)]
                junk, junkr = c.sb(st, "fjunk", [128, D], F32)
                ss = [c.sb(st, "fss%d" % i, [128, 1], F32) for i in range(2)]
                rs = [c.sb(st, "frs%d" % i, [128, 1], F32) for i in range(2)]
                for tt in range(NT):
                    (x_, xr), (s_, sr), (r_, rr_) = xt[tt % 2], ss[tt % 2], rs[tt % 2]
                    rows = slice(tt * 128, (tt + 1) * 128)
                    c.dma("sp", x_[:], xres[rows, :], [], [xr])
                    c.op("act", lambda e: e.activation(junk[:], x_[:], AF.Square, accum_out=s_[:]), [xr], [junkr, sr])
                    rms_rstd(s_[:], sr, r_[:], rr_, D)
                    c.op("dve", lambda e: e.scalar_tensor_tensor(x_[:], x_[:], r_[:], gf[:], ALU.mult, ALU.mult), [xr, rr_, gfr], [xr])
                    c.dma("sp", xres[rows, :], x_[:], [xr], [DR()])
                c.barrier()

        allst = stages if stages is not None else ("proj", "sb", "nsa", "hgrn", "merge", "peer", "final")
        for l in range(L):
            if "proj" in allst:
                stage_proj(l)
            if "sb" in allst:
                stage_sb(l)
            if "nsa" in allst:
                stage_nsa(l)
            if "hgrn" in allst:
                stage_hgrn(l)
            if "merge" in allst:
                stage_merge(l)
            if "xdbg" in debug and l == 0:
                c.dma("sp", S["xdbg"], xres, [], [DR()])
                c.barrier()
            if "peer" in allst:
                stage_peer(l)
        if "final" in allst:
            stage_final()
        c.barrier()
        build.n_ins = c.n_ins
    return nc


_CACHE = {}


def kernel(**inputs):
    cfg = Cfg(T=2048, NB=2, L=4)
    ncores = 8
    if "nc" not in _CACHE:
        _CACHE["nc"] = build(cfg)
    nc = _CACHE["nc"]
    consts = make_consts(cfg)
    x = np.ascontiguousarray(np.asarray(inputs["x"], dtype=np.float32))
    shared = {k: np.ascontiguousarray(np.asarray(inputs[k], dtype=np.float32)) for k in WEIGHT_SHAPES(cfg.L)}
    shared.update(consts)
    in_maps = []
    for i in range(ncores):
        m = dict(shared)
        m["x"] = x[i * cfg.NB:(i + 1) * cfg.NB].reshape(cfg.N, D)
        in_maps.append(m)
    res = run_bass_kernel_spmd(nc, in_maps, core_ids=list(range(ncores)))
    out = np.concatenate([np.asarray(r["out"]).reshape(cfg.NB, cfg.T, D) for r in res.results], axis=0)
    return out.astype(np.float32)
```
